# Optimizing a Trainium2 kernel written in Bass

```python
import jax, jax.numpy as jnp
from jax import lax
import numpy as np

D_MODEL = 1024
BATCH = 8
SEQ = 2048
DEPTH = 1
DEC_BATCH = 128
DEC_SEQ = 4
PAST_LEN = 16384
PAGE_SIZE = 128

MIX_WIDTH = D_MODEL
CONV_CH = MIX_WIDTH // 2
CONV_GROUPS = 8
MLP_CH = MIX_WIDTH - CONV_CH
MLP_HEADS = 8
MLP_HEAD_DIM = MLP_CH // MLP_HEADS
CONV_K = 3
CHUNK = 128
EPS = 1e-6
PROJ_WIDTH = 4 * CONV_CH + 3 * MLP_CH

kernel_name = "hybrid_shortconv_chunkmlp_adaln_step"


def rms_norm(x, g):
    xf = x.astype(jnp.float32)
    out = xf * lax.rsqrt(jnp.mean(xf * xf, axis=-1, keepdims=True) + EPS) * g.astype(jnp.float32)
    return out.astype(x.dtype)


def layer_norm(x, g, b):
    xf = x.astype(jnp.float32)
    mu = jnp.mean(xf, axis=-1, keepdims=True)
    xc = xf - mu
    var = jnp.mean(xc * xc, axis=-1, keepdims=True)
    out = xc * lax.rsqrt(var + EPS) * g.astype(jnp.float32) + b.astype(jnp.float32)
    return out.astype(x.dtype)


def chunk_spatial_mix(v, w_s, b_s):
    bsz, t, _ = v.shape
    n_chunks = -(-t // CHUNK)
    pad = n_chunks * CHUNK - t
    vp = jnp.pad(v, ((0, 0), (0, pad), (0, 0))).reshape(bsz, n_chunks, CHUNK, MLP_HEADS, MLP_HEAD_DIM)
    mask = jnp.tril(jnp.ones((CHUNK, CHUNK), dtype=bool))
    w = jnp.where(mask[None], w_s, jnp.zeros((), w_s.dtype))
    out = jnp.einsum("hts,bnshd->bnthd", w, vp) + b_s.T[None, None, :, :, None]
    return out.reshape(bsz, n_chunks * CHUNK, MLP_CH)[:, :t]


def mixer_layer(x, c, conv_buf, w_ada, b_ada, g_norm, w_in, conv_w, g_v, b_v, w_s, b_s, w_out):
    t = x.shape[1]
    mod = jax.nn.silu(c) @ w_ada + b_ada
    shift, scale, gate = jnp.split(mod[:, None, :], 3, axis=-1)
    h = rms_norm(x, g_norm) * (1 + scale) + shift
    p = h @ w_in
    splits = [CONV_CH, 2 * CONV_CH, 3 * CONV_CH, 4 * CONV_CH,
              4 * CONV_CH + MLP_CH, 4 * CONV_CH + 2 * MLP_CH]
    h_a, b_a, c_a, z_a, u_b, v_b, z_b = jnp.split(p, splits, axis=-1)
    s = c_a * h_a
    s_pad = jnp.concatenate([conv_buf.astype(s.dtype), s], axis=1)
    conv = conv_w[0] * s_pad[:, 0:t]
    for k in range(1, CONV_K):
        conv = conv + conv_w[k] * s_pad[:, k:k + t]
    y_a = b_a * conv * jax.nn.silu(z_a)
    new_buf = s_pad[:, t:]
    v_n = layer_norm(v_b, g_v, b_v)
    y_b = u_b * chunk_spatial_mix(v_n, w_s, b_s) * jax.nn.silu(z_b)
    y = jnp.concatenate([y_a, y_b], axis=-1) @ w_out
    return x + gate * y, new_buf, v_n


def setup_inputs(seed: int = 0) -> dict:
    key = jax.random.key(seed)
    ks = jax.random.split(key, 20)
    f = jnp.float32
    nrm = lambda k, shape: jax.random.normal(k, shape, f)
    inputs = {
        "x_prompt": nrm(ks[0], (BATCH, SEQ, D_MODEL)),
        "x_sample": nrm(ks[1], (DEC_BATCH, DEC_SEQ, D_MODEL)),
        "state_conv": nrm(ks[2], (DEPTH, DEC_BATCH, CONV_K - 1, CONV_CH)),
        "c_prompt": nrm(ks[3], (BATCH, D_MODEL)),
        "c_sample": nrm(ks[4], (DEC_BATCH, D_MODEL)),
        "w_ada": nrm(ks[5], (DEPTH, D_MODEL, 3 * D_MODEL)) * (0.5 * D_MODEL ** -0.5),
        "b_ada": nrm(ks[6], (DEPTH, 3 * D_MODEL)) * 0.02,
        "g_norm": 1.0 + 0.02 * nrm(ks[7], (DEPTH, D_MODEL)),
        "w_in": nrm(ks[8], (DEPTH, D_MODEL, PROJ_WIDTH)) * D_MODEL ** -0.5,
        "conv_w": nrm(ks[9], (DEPTH, CONV_K, CONV_CH)) * CONV_K ** -0.5,
        "g_v": 1.0 + 0.02 * nrm(ks[10], (DEPTH, MLP_CH)),
        "b_v": 0.02 * nrm(ks[11], (DEPTH, MLP_CH)),
        "w_s": nrm(ks[12], (DEPTH, MLP_HEADS, CHUNK, CHUNK)) * CHUNK ** -0.5,
        "b_s": 1.0 + 0.02 * nrm(ks[13], (DEPTH, MLP_HEADS, CHUNK)),
        "w_out": nrm(ks[14], (DEPTH, MIX_WIDTH, D_MODEL)) * MIX_WIDTH ** -0.5,
        "g_final": 1.0 + 0.02 * nrm(ks[15], (D_MODEL,)),
    }
    return inputs


def reference(x_prompt, x_sample, state_conv, c_prompt, c_sample, w_ada, b_ada, g_norm, w_in,
              conv_w, g_v, b_v, w_s, b_s, w_out, g_final):
    hp = x_prompt
    hs = x_sample
    conv_p, conv_s, v_s = [], [], []
    zero_buf = jnp.zeros((x_prompt.shape[0], CONV_K - 1, CONV_CH), x_prompt.dtype)
    for l in range(DEPTH):
        hp, buf_p, _ = mixer_layer(hp, c_prompt, zero_buf, w_ada[l], b_ada[l], g_norm[l], w_in[l],
                                   conv_w[l], g_v[l], b_v[l], w_s[l], b_s[l], w_out[l])
        hs, buf_s, vn_s = mixer_layer(hs, c_sample, state_conv[l], w_ada[l], b_ada[l], g_norm[l], w_in[l],
                                      conv_w[l], g_v[l], b_v[l], w_s[l], b_s[l], w_out[l])
        conv_p.append(buf_p)
        conv_s.append(buf_s)
        v_s.append(vn_s)
    y_prompt = rms_norm(hp, g_final)
    y_sample = rms_norm(hs, g_final)
    new_conv_prompt = jnp.stack(conv_p)
    new_conv_sample = jnp.stack(conv_s)
    new_v_sample = jnp.stack(v_s)
    return (y_prompt, y_sample, new_conv_prompt, new_conv_sample, new_v_sample)
```

```python
import numpy as np
import concourse.bass as bass
import concourse.mybir as mybir
from concourse.bass_utils import run_bass_kernel_spmd

F32 = mybir.dt.float32
BF16 = mybir.dt.bfloat16
ALU = mybir.AluOpType
AF = mybir.ActivationFunctionType

EPS = 1e-6
NCORES = 8
D = 1024
SEQ = 2048
NSAMP = 64
PROJ = 3584


class Op:
    __slots__ = ("eng", "fn", "deps", "is_dma", "needed", "sem", "val", "prev")

    def __init__(self, eng, fn, is_dma):
        self.eng = eng
        self.fn = fn
        self.is_dma = is_dma
        self.needed = False
        self.deps = []
        self.sem = None
        self.val = 0
        self.prev = None


class Sched:
    COMPUTE = ("pe", "act", "dve", "pool")
    QUEUES = ("sp", "act", "pool")
    ALL = ("pe", "act", "dve", "pool", "sp")

    def __init__(self, nc, ndma, strict=True):
        self.nc = nc
        self.streams = {e: [] for e in self.ALL}
        self.last_w = {}
        self.readers = {}
        self.ndma = ndma
        self.strict = strict

    def op(self, eng, fn, reads=(), writes=(), dma=False):
        o = Op(eng, fn, dma)
        self.nops = getattr(self, "nops", 0) + 1
        if self.nops > getattr(self, "limit", 10 ** 9):
            return o
        deps = {}
        for r in reads:
            w = self.last_w.get(r)
            if w is not None:
                deps[id(w)] = (w, "raw")
        for wr in writes:
            w = self.last_w.get(wr)
            if w is not None and id(w) not in deps:
                deps[id(w)] = (w, "waw")
            for rd in self.readers.get(wr, ()):
                if id(rd) not in deps:
                    deps[id(rd)] = (rd, "war")
        for d, kind in deps.values():
            if d is o:
                continue
            if (not d.is_dma) and (not dma) and d.eng == eng:
                if eng == "pe":
                    continue
                if kind != "raw" and not self.strict:
                    continue
            o.deps.append(d)
            d.needed = True
        for r in reads:
            self.readers.setdefault(r, []).append(o)
        for wr in writes:
            self.last_w[wr] = o
            self.readers[wr] = []
        self.streams[eng].append(o)
        return o

    def alias(self, new_res, old_res_list):
        rd = []
        for r in old_res_list:
            w = self.last_w.get(r)
            if w is not None:
                rd.append(w)
            rd.extend(self.readers.get(r, ()))
        self.readers.setdefault(new_res, []).extend(rd)

    def emit(self):
        nc = self.nc
        sems = {e: nc.alloc_semaphore(name=f"sem_{e}") for e in self.COMPUTE}
        dsems = {q: [nc.alloc_semaphore(name=f"dsem_{q}_{i}") for i in range(self.ndma[q])]
                 for q in self.QUEUES}
        for e in self.ALL:
            c = 0
            dmas = []
            for o in self.streams[e]:
                if o.is_dma:
                    n = self.ndma[e]
                    di = len(dmas)
                    o.sem = dsems[e][di % n]
                    o.val = 16 * (di // n + 1)
                    o.prev = dmas[di - n] if di >= n else None
                    dmas.append(o)
                elif o.needed:
                    c += 1
                    o.sem = sems[e]
                    o.val = c
        streams = self.streams
        stats = {}

        def run(e, eng):
            seen = {}
            nw = 0
            for o in streams[e]:
                need = {}
                waits = list(o.deps)
                if o.is_dma and o.prev is not None:
                    waits.append(o.prev)
                for d in waits:
                    key = d.sem.num
                    if key not in need or need[key][1] < d.val:
                        need[key] = (d.sem, d.val)
                for key, (sem, val) in need.items():
                    if seen.get(key, 0) >= val:
                        continue
                    eng.wait_ge(sem, val)
                    seen[key] = val
                    nw += 1
                ins = o.fn(eng)
                if o.is_dma:
                    ins.then_inc(o.sem, 16)
                elif o.needed:
                    ins.then_inc(o.sem, 1)
            stats[e] = (len(streams[e]), nw)

        with nc.Block() as block:
            @block.tensor
            def _(eng):
                run("pe", eng)

            @block.scalar
            def _(eng):
                run("act", eng)

            @block.vector
            def _(eng):
                run("dve", eng)

            @block.gpsimd
            def _(eng):
                run("pool", eng)

            @block.sync
            def _(eng):
                run("sp", eng)
                for q in self.QUEUES:
                    last = {}
                    for o in streams[q]:
                        if o.is_dma:
                            last[o.sem.num] = (o.sem, o.val)
                    for sem, val in last.values():
                        eng.wait_ge(sem, val)
        return stats


class Alloc:
    def __init__(self, nc):
        self.nc = nc
        self.off = (nc.sbuf_base + 63) // 64 * 64
        self.top = nc.sbuf_top
        self.hi = self.off

    def __call__(self, name, shape, dtype):
        isz = 2 if dtype == BF16 else 4
        size = isz
        for s in shape[1:]:
            size *= s
        off = self.off
        self.off += (size + 63) // 64 * 64
        self.hi = max(self.hi, self.off)
        assert self.off <= self.top, f"SBUF overflow at {name}: {self.off} > {self.top}"
        return self.nc.alloc_sbuf_tensor_at(name, list(shape), dtype, offset=off)


class Bank:
    def __init__(self, t, res):
        self.t = t
        self.res = res


def build_nc(limit=None):
    nc = bass.Bass("TRN2", target_bir_lowering=False)

    def din(name, shape):
        return nc.dram_tensor(name, list(shape), F32, kind="ExternalInput").ap()

    def dout(name, shape):
        return nc.dram_tensor(name, list(shape), F32, kind="ExternalOutput").ap()

    x_p = din("x_p", [SEQ, D])
    x_s = din("x_s", [NSAMP, D])
    state = din("state", [32, 512])
    c_all = din("c_all", [17, D])
    w_ada = din("w_ada", [D, 3 * D])
    b_ada = din("b_ada", [3 * D])
    g_norm = din("g_norm", [D])
    w_in = din("w_in", [D, PROJ])
    conv_w = din("conv_w", [3, 512])
    g_v = din("g_v", [512])
    b_v = din("b_v", [512])
    w_s = din("w_s", [8, 128, 128])
    b_s = din("b_s", [8, 128])
    w_out = din("w_out", [D, D])
    g_final = din("g_final", [D])
    y_p = dout("y_p", [SEQ, D])
    y_s = dout("y_s", [NSAMP, D])
    ncp = dout("ncp", [2, 512])
    ncs = dout("ncs", [32, 512])
    nvs = dout("nvs", [NSAMP, 512])

    S = Sched(nc, ndma={"sp": 8, "act": 20, "pool": 40}, strict=True)
    if limit is not None:
        S.limit = limit
    A = Alloc(nc)

    def op(eng, fn, r=(), w=(), dma=False):
        return S.op(eng, fn, reads=r, writes=w, dma=dma)

    win_bf = A("win_bf", [128, 8, PROJ], BF16)
    wout_bf = A("wout_bf", [128, 8, D], BF16)
    hT = [A(f"hT{i}", [128, 8, 512], BF16) for i in range(2)]
    xa = [A(f"xa{i}", [128, D], F32) for i in range(2)]
    xn = [A(f"xn{i}", [128, D], BF16) for i in range(2)]
    junk = A("junk", [128, D], BF16)
    spad = [A(f"spad{j}", [128, 514], F32) for j in range(4)]
    ssamp = [A(f"ssamp{j}", [128, 16, 6], F32) for j in range(4)]
    gv_bc = A("gv_bc", [128, 512], F32)
    bv_bc = A("bv_bc", [128, 512], F32)
    gf_bc = A("gf_bc", [128, D], F32)
    gate_bc_p = A("gate_bc_p", [128, D], F32)
    gate_bc_s = A("gate_bc_s", [128, D], F32)
    bsT = A("bsT", [128, 4, 128], F32)
    Wt = A("Wt", [128, 8, 128], BF16)
    Wblk = A("Wblk", [64, 8, 64], BF16)
    ident_bf = A("ident_bf", [128, 128], BF16)
    ident_f = A("ident_f", [128, 128], F32)
    modT = A("modT", [128, 24, 17], F32)
    aT = A("aT", [128, 8, 17], F32)
    gT = A("gT", [128, 8], F32)
    badaT = A("badaT", [128, 24], F32)
    cwT = A("cwT", [128, 4, 3], F32)
    selP = A("selP", [17, 128], F32)
    selS = A("selS", [17, 64], F32)
    stat = A("stat", [128, 64], F32)
    mhalf = A("mhalf", [128, 4], F32)
    st6 = [A(f"st6_{i}", [128, 6], F32) for i in range(4)]
    ncs_sb = A("ncs_sb", [32, 512], F32)
    ncp_sb = ncs_sb
    nsT = A("nsT", [128, 4, 32], F32)
    vnS = A("vnS", [64, 512], F32)
    gvT = A("gvT", [128, 4], F32)
    bvT = A("bvT", [128, 4], F32)
    bias2 = A("bias2", [128, 4, 128], F32)
    ones_bf = A("ones_bf", [128, 64], BF16)
    hT_s = A("hT_s", [128, 8, 64], BF16)
    yT_s = A("yT_s", [128, 8, 64], BF16)
    vn_s = A("vn_s", [64, 512], BF16)
    NTMPS = 6
    tmps = [A(f"tmps{i}", [128, 64], F32) for i in range(NTMPS)]

    silucT = A("silucT", [128, 8, 17], BF16)
    mark = A.off
    c_sb = A("c_sb", [17, D], F32)
    sc = A("sc", [17, D], F32)
    ws_nat = A("ws_nat", [128, 8, 128], F32)
    wsT_f = A("wsT_f", [128, 8, 128], F32)
    state_sb = A("state_sb", [32, 512], F32)
    wada0 = A("wada0", [128, 8, 512], BF16)
    late_mark = A.off
    gate_rows = A("gate_rows", [17, D], F32)
    wada1 = A("wada1", [128, 8, 512], BF16)
    wada2 = A("wada2", [128, 8, 512], BF16)
    wada = [wada0, wada1, wada2]
    startup_end = A.off
    early_startup_res = ["c_sb", "sc", "ws_nat", ("wsT_f", 0), ("wsT_f", 1), "wsT_m", "state_sb", ("wada", 0)]
    late_startup_res = [("gate_rows", 0), ("gate_rows", 1), ("wada", 1), ("wada", 2), "silucT"]
    A.off = mark
    NTMP = 9
    NOB = 3
    NVZ = 4
    tmpb = [A(f"tmp{i}", [128, 512], F32) for i in range(NTMP)]
    vz = [A(f"vz{i}", [128, 512], F32) for i in range(NVZ)]
    assert A.off <= late_mark, (A.off, late_mark)
    A.off = late_mark
    xc = [A(f"xc{i}", [128, D], F32) for i in range(2)]
    ob = [A(f"ob{i}", [128, D], F32) for i in range(NOB)]
    A.off = max(A.off, startup_end)
    yT = A("yT", [128, 8, 512], BF16)
    vn_bf = [A(f"vn_bf{i}", [128, 512], BF16) for i in range(4)]
    early_main = [("tmp", i) for i in range(NTMP)] + [("vz", i) for i in range(NVZ)]
    late_main = [("xc", i) for i in range(2)] + [("o", i) for i in range(NOB)]

    pT = [nc.alloc_psum_tensor(f"pT{i}", [128, 8, 128], BF16) for i in range(2)]
    NRING = 6
    pbank = [Bank(nc.alloc_psum_tensor(f"pb{i}", [128, 512], F32), ("bank", i)) for i in range(NRING)]
    ring_i = [0]

    def nbank():
        b = pbank[ring_i[0] % NRING]
        ring_i[0] += 1
        return b

    tmp_i = [0]

    def ntmp():
        i = tmp_i[0] % NTMP
        tmp_i[0] += 1
        return tmpb[i], ("tmp", i)

    op("pool", lambda e: e.memset(ident_f[:], 0.0), w=["ident_f"])
    op("pool", lambda e: e.affine_select(out=ident_f[:], in_=ident_f[:], compare_op=ALU.not_equal,
                                         fill=1.0, base=0, pattern=[[-1, 128]], channel_multiplier=1),
       r=["ident_f"], w=["ident_f"])
    op("pool", lambda e: e.tensor_copy(out=ident_bf[:], in_=ident_f[:]), r=["ident_f"], w=["ident_bf"])
    op("pool", lambda e: e.memset(selP[:], 0.0), w=["selP"])
    op("pool", lambda e: e.memset(selP[0:1, :], 1.0), r=["selP"], w=["selP"])
    op("pool", lambda e: e.memset(selS[:], 1.0), w=["selS"])
    op("pool", lambda e: e.affine_select(out=selS[:], in_=selS[:], compare_op=ALU.is_ge, fill=0.0,
                                         base=4, pattern=[[1, 64]], channel_multiplier=-4),
       r=["selS"], w=["selS"])
    op("pool", lambda e: e.affine_select(out=selS[:], in_=selS[:], compare_op=ALU.is_ge, fill=0.0,
                                         base=-1, pattern=[[-1, 64]], channel_multiplier=4),
       r=["selS"], w=["selS"])
    for j in range(4):
        op("pool", lambda e, j=j: e.memset(spad[j][:, 0:2], 0.0), w=[("spadH", j)])
    op("pool", lambda e: e.memset(Wblk[:], 0.0), w=["Wblk"])
    op("pool", lambda e: e.memset(mhalf[:], -0.5), w=["mhalf"])
    op("pool", lambda e: e.memset(ones_bf[:], 1.0), w=["ones_bf"])

    def sp_load(dst, src, res, nonc=False, q="act"):
        if nonc:
            op(q, lambda e: e.dma_start(out=dst, in_=src, allow_slow_non_contiguous=True), w=[res], dma=True)
        else:
            op(q, lambda e: e.dma_start(out=dst, in_=src), w=[res], dma=True)

    sp_load(c_sb[:], c_all[:, :], "c_sb", q="sp")
    def small_loads_a():
        sp_load(gT[:], g_norm.rearrange("(k p) -> p k", p=128), "gT", nonc=True)
        sp_load(badaT[:], b_ada.rearrange("(c p) -> p c", p=128), "badaT", nonc=True)
        for k3 in range(3):
            sp_load(cwT[:, :, k3], conv_w[k3, :].rearrange("(j p) -> p j", p=128), ("cwT", k3), nonc=True)
    cw_res = [("cwT", k3) for k3 in range(3)]

    w_ada_v = w_ada.rearrange("(k p) n -> p k n", p=128)
    w_in_v = w_in.rearrange("(k p) n -> p k n", p=128)
    w_out_v = w_out.rearrange("(k p) n -> p k n", p=128)

    def wada_dma(blk):
        slot = blk % 3
        op("pool", lambda e: e.dma_start(out=wada[slot][:], in_=w_ada_v[:, :, blk * 512:(blk + 1) * 512]),
           w=[("wada", slot)], dma=True)

    for blk in range(3):
        wada_dma(blk)

    def small_loads_b():
        sp_load(gvT[:], g_v.rearrange("(j p) -> p j", p=128), "gvT", nonc=True)
        sp_load(bvT[:], b_v.rearrange("(j p) -> p j", p=128), "bvT", nonc=True)
        sp_load(gv_bc[:], g_v.partition_broadcast(128), "gv_bc")
        sp_load(bv_bc[:], b_v.partition_broadcast(128), "bv_bc")
        sp_load(gf_bc[:], g_final.partition_broadcast(128), "gf_bc")
        for h in range(8):
            sp_load(bsT[64 * (h % 2):64 * (h % 2) + 64, h // 2, :], b_s[h, :].partition_broadcast(64), ("bsT", h))
        sp_load(ws_nat[:], w_s.rearrange("h t s -> t h s"), "ws_nat")
        sp_load(state_sb[:], state[:, :], "state_sb")

    op("act", lambda e: e.activation(out=sc[:], in_=c_sb[:], func=AF.Silu), r=["c_sb"], w=["sc"])
    bk = nbank()

    def fn(pe, bk=bk):
        for k in range(8):
            i = pe.transpose(out=bk.t[:, k * 17:(k + 1) * 17], in_=sc[:, k * 128:(k + 1) * 128],
                             identity=ident_f[0:17, 0:17])
        return i
    op("pe", fn, r=["sc", "ident_f"], w=[bk.res])
    op("dve", lambda e, bk=bk: e.tensor_copy(out=silucT[:], in_=bk.t[:, 0:136].rearrange("p (k j) -> p k j", j=17)),
       r=[bk.res], w=["silucT"])

    modps = nbank()
    modps2 = nbank()

    def ada_pe(blk):
        slot = blk % 3
        tgt = modps if blk < 4 else modps2

        def fn(pe):
            for c4 in range(4):
                c = blk * 4 + c4
                cc = c if blk < 4 else c - 16
                for k in range(8):
                    i = pe.matmul(tgt.t[:, cc * 17:(cc + 1) * 17], lhsT=wada[slot][:, k, c4 * 128:(c4 + 1) * 128],
                                  rhs=silucT[:, k, :], start=(k == 0), stop=(k == 7))
            return i
        op("pe", fn, r=[("wada", slot), "silucT"], w=[(tgt.res, blk)] if False else [tgt.res])

    m_order = [20, 21, 22, 23]
    for j in range(4):
        m_order += [j, 8 + j, 4 + j, 12 + j]
    for j in range(4):
        m_order += [16 + j, 24 + j]

    def win_dma(m):
        op("pool", lambda e: e.dma_start(out=win_bf[:, :, m * 128:(m + 1) * 128], in_=w_in_v[:, :, m * 128:(m + 1) * 128]),
           w=[("win", m)], dma=True)

    def modT_ops():
        op("dve", lambda e: e.tensor_tensor(out=modT[:, 0:16, :],
                                            in0=modps.t[:, 0:272].rearrange("p (c j) -> p c j", j=17),
                                            in1=badaT[:, 0:16].unsqueeze(2).to_broadcast([128, 16, 17]), op=ALU.add),
           r=[modps.res, "badaT"], w=["modT_ss"])
        op("dve", lambda e: e.scalar_tensor_tensor(out=aT[:], in0=modT[:, 8:16, :], scalar=1.0,
                                                   in1=gT[:].unsqueeze(2).to_broadcast([128, 8, 17]),
                                                   op0=ALU.add, op1=ALU.mult),
           r=["modT_ss", "gT"], w=["aT"])

    def wout_dma(h2):
        op("pool", lambda e: e.dma_start(out=wout_bf[:, :, h2 * 512:(h2 + 1) * 512],
                                         in_=w_out_v[:, :, h2 * 512:(h2 + 1) * 512]),
           w=[("wout", h2)], dma=True)

    wblk_res = [("Wblk", j) for j in range(16)]

    def startup_rest():
        for half in range(2):
            bk = nbank()

            def fn(pe, bk=bk, half=half):
                for hh in range(4):
                    i = pe.transpose(out=bk.t[:, hh * 128:(hh + 1) * 128], in_=ws_nat[:, half * 4 + hh, :],
                                     identity=ident_f[:])
                return i
            op("pe", fn, r=["ws_nat", "ident_f"], w=[bk.res])
            op("act", lambda e, bk=bk, half=half: e.copy(out=wsT_f[:, half * 4:(half + 1) * 4, :],
                                                         in_=bk.t[:].rearrange("p (h t) -> p h t", t=128)),
               r=[bk.res], w=[("wsT_f", half)])
        op("pool", lambda e: e.affine_select(out=wsT_f[:], in_=wsT_f[:], compare_op=ALU.is_ge, fill=0.0, base=0,
                                             pattern=[[0, 8], [1, 128]], channel_multiplier=-1),
           r=[("wsT_f", 0), ("wsT_f", 1)], w=["wsT_m"])
        op("pool", lambda e: e.tensor_copy(out=Wt[:], in_=wsT_f[:]), r=["wsT_m"], w=["Wt"])
        for j in range(16):
            op("sp", lambda e, j=j: e.dma_start(out=Wblk[4 * j:4 * j + 4, :, 4 * j:4 * j + 4], in_=Wt[0:4, :, 0:4],
                                                 allow_slow_non_contiguous=True),
               r=["Wt", "Wblk"], w=[("Wblk", j)], dma=True)

        bk = nbank()

        def fn(pe, bk=bk):
            for j in range(4):
                i = pe.transpose(out=bk.t[:, j * 32:(j + 1) * 32], in_=state_sb[0:32, j * 128:(j + 1) * 128],
                                 identity=ident_f[0:32, 0:32])
            return i
        op("pe", fn, r=["state_sb", "ident_f"], w=[bk.res])

        def fn(e, bk=bk):
            for j in range(4):
                i = e.tensor_copy(out=ssamp[j][:, :, 0:2],
                                  in_=bk.t[:, j * 32:(j + 1) * 32].rearrange("p (s r) -> p s r", r=2))
            return i
        op("dve", fn, r=[bk.res], w=[("ssampH", j) for j in range(4)])

        for rname in early_main:
            S.alias(rname, early_startup_res)

    def bias2_ops():
        bk = nbank()

        def fn(pe):
            for h in range(8):
                ii = pe.matmul(bk.t[64 * (h % 2):64 * (h % 2) + 64, (h // 2) * 128:(h // 2 + 1) * 128],
                               lhsT=ones_bf[:, 0:64], rhs=Wt[:, h, :], start=True, stop=True)
            return ii
        op("pe", fn, r=["Wt", "ones_bf"], w=[bk.res])

        def fd(e):
            for jj in range(4):
                ii = e.scalar_tensor_tensor(out=bias2[:, jj, :], in0=bk.t[:, jj * 128:(jj + 1) * 128],
                                            scalar=bvT[:, jj:jj + 1], in1=bsT[:, jj, :], op0=ALU.mult, op1=ALU.add)
            return ii
        op("dve", fd, r=[bk.res, "bvT"] + [("bsT", h) for h in range(8)], w=["bias2"])

    gate_state = {}

    def gate_stage1():
        ada_pe(4)
        ada_pe(5)
        op("dve", lambda e: e.tensor_tensor(out=modT[:, 16:24, :],
                                            in0=modps2.t[:, 0:136].rearrange("p (c j) -> p c j", j=17),
                                            in1=badaT[:, 16:24].unsqueeze(2).to_broadcast([128, 8, 17]), op=ALU.add),
           r=[modps2.res, "badaT"], w=["modT_g"])

    def gate_stage2():
        gbk = [nbank(), nbank()]
        for half in range(2):
            def fn(pe, half=half):
                for kk in range(4):
                    i = pe.transpose(out=gbk[half].t[0:17, kk * 128:(kk + 1) * 128], in_=modT[:, 16 + half * 4 + kk, :],
                                     identity=ident_f[:])
                return i
            op("pe", fn, r=["modT_g", "ident_f"], w=[gbk[half].res])
            op("act", lambda e, half=half: e.copy(out=gate_rows[:, half * 512:(half + 1) * 512], in_=gbk[half].t[0:17, :]),
               r=[gbk[half].res], w=[("gate_rows", half)])

    def gate_stage3():
        for half in range(2):
            bk = nbank()
            op("pe", lambda pe, bk=bk, half=half: pe.matmul(bk.t[:, :], lhsT=selP[:, :],
                                                           rhs=gate_rows[:, half * 512:(half + 1) * 512],
                                                           start=True, stop=True),
               r=["selP", ("gate_rows", half)], w=[bk.res])
            op("act", lambda e, bk=bk, half=half: e.copy(out=gate_bc_p[:, half * 512:(half + 1) * 512], in_=bk.t[:, :]),
               r=[bk.res], w=[("gate_bc_p", half)])
            bk = nbank()
            op("pe", lambda pe, bk=bk, half=half: pe.matmul(bk.t[0:64, :], lhsT=selS[:, :],
                                                           rhs=gate_rows[:, half * 512:(half + 1) * 512],
                                                           start=True, stop=True),
               r=["selS", ("gate_rows", half)], w=[bk.res])
            op("act", lambda e, bk=bk, half=half: e.copy(out=gate_bc_s[0:64, half * 512:(half + 1) * 512], in_=bk.t[0:64, :]),
               r=[bk.res], w=[("gate_bc_s", half)])
        for rname in late_main:
            S.alias(rname, late_startup_res)

    class Blk:
        pass

    def mkblk(b):
        B = Blk()
        B.idx = b
        if b < 4:
            B.N, B.tiles, B.t0, B.sample = 512, [(i, 128) for i in range(4)], b * 512, False
            B.hT, B.hkey, B.yT, B.ykey, B.vn, B.vkey = hT[b % 2], b % 2, yT, "p", vn_bf, "p"
        else:
            B.N, B.tiles, B.t0, B.sample = 64, [(0, 64)], 0, True
            B.hT, B.hkey, B.yT, B.ykey, B.vn, B.vkey = hT_s, "s", yT_s, "s", [vn_s], "s"
        return B

    PB = [mkblk(b) for b in range(4)]
    SB = mkblk(4)

    cnt = {"xa": 0, "pT": 0, "xc": 0, "o": 0, "vz": 0, "st": 0}
    st_i = [0]

    def nstat(n=1):
        i = st_i[0] % 60
        if i + n > 60:
            i = 0
        st_i[0] = i + n
        return i

    tmps_i = [0]

    def ntmp_for(B):
        if not B.sample:
            return ntmp()
        i = tmps_i[0] % NTMPS
        tmps_i[0] += 1
        return tmps[i], ("tmps", i)

    out_res = []
    a_state = {}

    def A_pre(B, ti):
        i, P = B.tiles[ti]
        src = x_s[0:64, :] if B.sample else x_p[B.t0 + i * 128: B.t0 + (i + 1) * 128, :]
        ra = cnt["xa"] % 2
        cnt["xa"] += 1
        sidx = nstat(2)
        ss = stat[0:P, sidx:sidx + 1]
        rs = stat[0:P, sidx + 1:sidx + 2]
        op("sp", lambda e: e.dma_start(out=xa[ra][0:P, :], in_=src), w=[("xa", ra)], dma=True)
        op("act", lambda e: e.activation(out=junk[0:P, :], in_=xa[ra][0:P, :], func=AF.Square, accum_out=ss),
           r=[("xa", ra)], w=[("stat", sidx), "junk"])
        op("dve", lambda e: e.tensor_scalar(out=rs, in0=ss, scalar1=1.0 / D, scalar2=EPS, op0=ALU.mult, op1=ALU.add),
           r=[("stat", sidx)], w=[("stat", sidx + 1)])
        op("pool", lambda e: e.tensor_tensor(out=rs, in0=rs, in1=mhalf[0:P, 0:1], op=ALU.pow),
           r=[("stat", sidx + 1), "mhalf"], w=[("stat", sidx + 1)])
        op("dve", lambda e: e.tensor_scalar(out=xn[ra][0:P, :], in0=xa[ra][0:P, :], scalar1=rs, scalar2=None,
                                            op0=ALU.mult),
           r=[("xa", ra), ("stat", sidx + 1)], w=[("xn", ra)])
        a_state[(B.idx, ti)] = ra

    def A_post(B, ti):
        i, P = B.tiles[ti]
        ra = a_state[(B.idx, ti)]
        pa = cnt["pT"] % 2
        cnt["pT"] += 1
        hTb = B.hT

        def fn(pe):
            for k in range(8):
                ii = pe.transpose(out=pT[pa][:, k, 0:P], in_=xn[ra][0:P, k * 128:(k + 1) * 128],
                                  identity=ident_bf[0:P, 0:P])
            return ii
        op("pe", fn, r=[("xn", ra), "ident_bf"], w=[("pT", pa)])
        if not B.sample:
            def fd(e):
                for k in range(8):
                    ii = e.tensor_scalar(out=hTb[:, k, i * 128:(i + 1) * 128], in0=pT[pa][:, k, :],
                                         scalar1=aT[:, k, 0:1], scalar2=modT[:, k, 0:1], op0=ALU.mult, op1=ALU.add)
                return ii

            def fa(e):
                for k in range(8):
                    ii = e.activation(out=hTb[:, k, i * 128:(i + 1) * 128], in_=pT[pa][:, k, :], func=AF.Identity,
                                      bias=modT[:, k, 0:1], scale=aT[:, k, 0:1])
                return ii
            if ti % 2 == 0:
                op("dve", fd, r=[("pT", pa), "aT", "modT_ss"], w=[("hT", B.hkey, i)])
            else:
                op("act", fa, r=[("pT", pa), "aT", "modT_ss"], w=[("hT", B.hkey, i)])
        else:
            tb, tres = ntmp()
            tv = tb[:, :].rearrange("p (k s i) -> p k s i", k=8, i=4)
            op("dve", lambda e: e.tensor_tensor(out=tv, in0=pT[pa][:, :, 0:64].rearrange("p k (s i) -> p k s i", i=4),
                                                in1=aT[:, :, 1:17].unsqueeze(3).to_broadcast([128, 8, 16, 4]),
                                                op=ALU.mult),
               r=[("pT", pa), "aT"], w=[tres])
            op("dve", lambda e: e.tensor_tensor(out=hTb[:, :, 0:64].rearrange("p k (s i) -> p k s i", i=4), in0=tv,
                                                in1=modT[:, 0:8, 1:17].unsqueeze(3).to_broadcast([128, 8, 16, 4]),
                                                op=ALU.add),
               r=[tres, "modT_ss"], w=[("hT", B.hkey, 0)])

    def hT_res(B):
        return [("hT", B.hkey, i) for (i, P) in B.tiles]

    def mm_proj(B, m):
        N = B.N
        bk = nbank()

        def fn(pe):
            for k in range(8):
                ii = pe.matmul(bk.t[:, 0:N], lhsT=win_bf[:, k, m * 128:(m + 1) * 128], rhs=B.hT[:, k, 0:N],
                               start=(k == 0), stop=(k == 7))
            return ii
        op("pe", fn, r=[("win", m)] + hT_res(B), w=[bk.res])
        return bk

    def B_vpath(B):
        nt = len(B.tiles)
        P = B.tiles[0][1]
        sidx = nstat(16)
        mvall = stat[0:P, sidx:sidx + 8]
        mv3 = mvall.rearrange("p (t c) -> p t c", c=2)
        rstd4 = stat[0:P, sidx + 8:sidx + 8 + nt]
        nmr4 = stat[0:P, sidx + 12:sidx + 12 + nt]
        mvres = [("stat", sidx + c) for c in range(8)]
        rres = [("stat", sidx + 8 + c) for c in range(4)]
        nres = [("stat", sidx + 12 + c) for c in range(4)]
        vis = []
        for idx, (i, P_) in enumerate(B.tiles):
            bk = nbank()

            def fn(pe, bk=bk, i=i):
                for k in range(8):
                    ii = pe.matmul(bk.t[0:P, :], lhsT=B.hT[:, k, i * 128:i * 128 + P], rhs=win_bf[:, k, 2560:3072],
                                   start=(k == 0), stop=(k == 7))
                return ii
            op("pe", fn, r=[("win", m) for m in (20, 21, 22, 23)] + hT_res(B), w=[bk.res])
            s6 = cnt["st"] % 4
            cnt["st"] += 1
            vi = cnt["vz"] % NVZ
            cnt["vz"] += 1
            vis.append(vi)
            op("act", lambda e, bk=bk, vi=vi: e.copy(out=vz[vi][0:P, :], in_=bk.t[0:P, :]),
               r=[bk.res], w=[("vz", vi)])
            op("dve", lambda e, vi=vi, s6=s6: e.bn_stats(out=st6[s6][0:P, :], in_=vz[vi][0:P, :]),
               r=[("vz", vi)], w=[("st6", s6)])
            op("dve", lambda e, s6=s6, idx=idx: e.bn_aggr(out=mvall[:, 2 * idx:2 * idx + 2], in_=st6[s6][0:P, :]),
               r=[("st6", s6)], w=[mvres[2 * idx], mvres[2 * idx + 1]])
        op("dve", lambda e: e.tensor_scalar(out=rstd4, in0=mv3[:, 0:nt, 1], scalar1=EPS, scalar2=None, op0=ALU.add),
           r=mvres[0:2 * nt], w=rres)
        op("pool", lambda e: e.tensor_tensor(out=rstd4, in0=rstd4, in1=mhalf[0:P, 0:nt], op=ALU.pow),
           r=rres + ["mhalf"], w=rres)
        op("dve", lambda e: e.scalar_tensor_tensor(out=nmr4, in0=mv3[:, 0:nt, 0], scalar=-1.0, in1=rstd4,
                                                   op0=ALU.mult, op1=ALU.mult),
           r=mvres[0:2 * nt] + rres, w=nres)
        for idx, (i, P_) in enumerate(B.tiles):
            vi = vis[idx]
            rstd = rstd4[:, idx:idx + 1]
            nmr = nmr4[:, idx:idx + 1]
            if not B.sample:
                op("act", lambda e, vi=vi, i=i, rstd=rstd, nmr=nmr: e.activation(
                    out=vn_bf[i][:, :], in_=vz[vi][:, :], func=AF.Identity, bias=nmr, scale=rstd),
                   r=[("vz", vi)] + rres + nres, w=[("vn", "p", i)])
            else:
                op("act", lambda e, vi=vi, rstd=rstd, nmr=nmr: e.activation(
                    out=vz[vi][0:P, :], in_=vz[vi][0:P, :], func=AF.Identity, bias=nmr, scale=rstd),
                   r=[("vz", vi)] + rres + nres, w=[("vz", vi)])
                op("pool", lambda e, vi=vi: e.tensor_tensor(out=vz[vi][0:P, :], in0=vz[vi][0:P, :],
                                                            in1=gv_bc[0:P, :], op=ALU.mult),
                   r=[("vz", vi), "gv_bc"], w=[("vz", vi)])
                op("pool", lambda e, vi=vi: e.tensor_tensor(out=vnS[0:64, :], in0=vz[vi][0:64, :], in1=bv_bc[0:64, :],
                                                            op=ALU.add),
                   r=[("vz", vi), "bv_bc"], w=["vnS"])
                op("act", lambda e: e.copy(out=vn_s[0:64, :], in_=vnS[0:64, :]), r=["vnS"], w=[("vn", "s", 0)])
                op("pool", lambda e: e.dma_start(out=nvs[:, :], in_=vnS[0:64, :]), r=["vnS"], w=["out_nvs"], dma=True)
                out_res.append("out_nvs")

    ga_state = {}

    def B_groupA1(B, j):
        N = B.N
        smp = B.sample
        bH = mm_proj(B, j)
        bC = mm_proj(B, 8 + j)
        t1, t1r = ntmp_for(B)
        op("act", lambda e: e.copy(out=t1[:, 0:N], in_=bH.t[:, 0:N]), r=[bH.res], w=[t1r])
        if not smp:
            if B.idx > 0:
                op("dve", lambda e: e.tensor_copy(out=spad[j][:, 0:2], in_=spad[j][:, 512:514]),
                   r=[("spadB", j)], w=[("spadH", j)])
            s_body = spad[j][:, 2:514]
            sres_b, sres_h = ("spadB", j), ("spadH", j)
            s0, s1, s2 = spad[j][:, 0:512], spad[j][:, 1:513], spad[j][:, 2:514]
            v3 = lambda ap: ap
        else:
            s_body = ssamp[j][:, :, 2:6]
            sres_b, sres_h = ("ssampB", j), ("ssampH", j)
            s0, s1, s2 = ssamp[j][:, :, 0:4], ssamp[j][:, :, 1:5], ssamp[j][:, :, 2:6]
            v3 = lambda ap: ap.rearrange("p (s i) -> p s i", i=4)
        op("dve", lambda e: e.tensor_tensor(out=s_body, in0=v3(bC.t[:, 0:N]), in1=v3(t1[:, 0:N]), op=ALU.mult),
           r=[bC.res, t1r], w=[sres_b])
        c1, c1r = ntmp_for(B)
        c2, c2r = ntmp_for(B)
        op("act", lambda e: e.activation(out=v3(c1[:, 0:N]), in_=s2, func=AF.Copy, scale=cwT[:, j, 2:3]),
           r=[sres_b] + cw_res, w=[c1r])
        op("dve", lambda e: e.scalar_tensor_tensor(out=v3(c2[:, 0:N]), in0=s1, scalar=cwT[:, j, 1:2], in1=v3(c1[:, 0:N]),
                                                   op0=ALU.mult, op1=ALU.add),
           r=[sres_b, sres_h, c1r] + cw_res, w=[c2r])
        op("dve", lambda e: e.scalar_tensor_tensor(out=v3(c1[:, 0:N]), in0=s0, scalar=cwT[:, j, 0:1], in1=v3(c2[:, 0:N]),
                                                   op0=ALU.mult, op1=ALU.add),
           r=[sres_b, sres_h, c2r] + cw_res, w=[c1r])
        ga_state[(B.idx, j)] = (c1, c1r, c2, c2r)

    def B_groupA2(B, j):
        N = B.N
        c1, c1r, c2, c2r = ga_state[(B.idx, j)]
        bB = mm_proj(B, 4 + j)
        bZ = mm_proj(B, 12 + j)
        sg, sgr = ntmp_for(B)
        op("act", lambda e: e.activation(out=sg[:, 0:N], in_=bZ.t[:, 0:N], func=AF.Silu), r=[bZ.res], w=[sgr])
        op("dve", lambda e: e.tensor_tensor(out=c2[:, 0:N], in0=bB.t[:, 0:N], in1=c1[:, 0:N], op=ALU.mult),
           r=[bB.res, c1r], w=[c2r])
        op("pool", lambda e: e.tensor_tensor(out=B.yT[:, j, 0:N], in0=c2[:, 0:N], in1=sg[:, 0:N], op=ALU.mult),
           r=[c2r, sgr], w=[("yT", B.ykey, j)])

    def B_newconv(B):
        if B.sample:
            def fn(e):
                for j in range(4):
                    ii = e.tensor_copy(out=nsT[:, j, :].rearrange("p (s r) -> p s r", r=2), in_=ssamp[j][:, :, 4:6])
                return ii
            op("dve", fn, r=[("ssampB", j) for j in range(4)], w=["nsT"])
            bk = nbank()

            def fn(pe):
                for j in range(4):
                    ii = pe.transpose(out=bk.t[0:32, j * 128:(j + 1) * 128], in_=nsT[:, j, :], identity=ident_f[:])
                return ii
            op("pe", fn, r=["nsT", "ident_f"], w=[bk.res])
            op("act", lambda e: e.copy(out=ncs_sb[:, :], in_=bk.t[0:32, :]), r=[bk.res], w=["ncs_sb"])
            op("pool", lambda e: e.dma_start(out=ncs[:, :], in_=ncs_sb[:, :]), r=["ncs_sb"], w=["out_ncs"], dma=True)
            out_res.append("out_ncs")
        elif B.idx == 3:
            bk = nbank()

            def fn(pe):
                for j in range(4):
                    ii = pe.transpose(out=bk.t[0:2, j * 128:(j + 1) * 128], in_=spad[j][:, 512:514], identity=ident_f[:])
                return ii
            op("pe", fn, r=[("spadB", j) for j in range(4)] + ["ident_f"], w=[bk.res])
            op("act", lambda e: e.copy(out=ncp_sb[0:2, :], in_=bk.t[0:2, :]), r=[bk.res], w=["ncs_sb"])
            op("pool", lambda e: e.dma_start(out=ncp[:, :], in_=ncp_sb[0:2, :]), r=["ncs_sb"], w=["out_ncp"], dma=True)
            out_res.append("out_ncp")

    def B_groupB(B, j):
        N = B.N
        smp = B.sample
        bM = nbank()
        if not smp:
            def fn(pe):
                for (i, P) in B.tiles:
                    for half in range(2):
                        h = 2 * j + half
                        ii = pe.matmul(bM.t[64 * half:64 * half + 64, i * 128:(i + 1) * 128],
                                       lhsT=vn_bf[i][:, h * 64:(h + 1) * 64], rhs=Wt[:, h, :], start=True, stop=True)
                return ii
            op("pe", fn, r=[("vn", "p", i) for i in range(4)] + ["Wt"], w=[bM.res])
        else:
            def fn(pe):
                for half in range(2):
                    h = 2 * j + half
                    ii = pe.matmul(bM.t[64 * half:64 * half + 64, 0:64], lhsT=vn_s[0:64, h * 64:(h + 1) * 64],
                                   rhs=Wblk[0:64, h, :], start=True, stop=True)
                return ii
            op("pe", fn, r=[("vn", "s", 0)] + wblk_res, w=[bM.res])
        bU = mm_proj(B, 16 + j)
        bZ = mm_proj(B, 24 + j)
        sg, sgr = ntmp_for(B)
        tU, tUr = ntmp_for(B)
        mb, mbr = ntmp_for(B)
        op("act", lambda e: e.activation(out=sg[:, 0:N], in_=bZ.t[:, 0:N], func=AF.Silu), r=[bZ.res], w=[sgr])
        op("dve", lambda e: e.tensor_tensor(out=tU[:, 0:N], in0=bU.t[:, 0:N], in1=sg[:, 0:N], op=ALU.mult),
           r=[bU.res, sgr], w=[tUr])
        if not smp:
            op("dve", lambda e: e.scalar_tensor_tensor(out=mb[:, 0:N].rearrange("p (i t) -> p i t", t=128),
                                                       in0=bM.t[:, 0:N].rearrange("p (i t) -> p i t", t=128),
                                                       scalar=gvT[:, j:j + 1],
                                                       in1=bias2[:, j, :].unsqueeze(1).to_broadcast([128, 4, 128]),
                                                       op0=ALU.mult, op1=ALU.add),
               r=[bM.res, "gvT", "bias2"], w=[mbr])
        else:
            op("dve", lambda e: e.tensor_tensor(out=mb[:, 0:N].rearrange("p (s i) -> p s i", i=4),
                                                in0=bM.t[:, 0:N].rearrange("p (s i) -> p s i", i=4),
                                                in1=bsT[:, j, 0:4].unsqueeze(1).to_broadcast([128, 16, 4]), op=ALU.add),
               r=[bM.res, ("bsT", 2 * j), ("bsT", 2 * j + 1)], w=[mbr])
        op("pool", lambda e: e.tensor_tensor(out=B.yT[:, 4 + j, 0:N], in0=tU[:, 0:N], in1=mb[:, 0:N], op=ALU.mult),
           r=[tUr, mbr], w=[("yT", B.ykey, 4 + j)])

    c_state = {}

    def C_tile(B, ti):
        i, P = B.tiles[ti]
        smp = B.sample
        src = x_s[0:64, :] if smp else x_p[B.t0 + i * 128: B.t0 + (i + 1) * 128, :]
        dst = y_s[0:64, :] if smp else y_p[B.t0 + i * 128: B.t0 + (i + 1) * 128, :]
        gbc = gate_bc_s if smp else gate_bc_p
        gres = [("gate_bc_s", 0), ("gate_bc_s", 1)] if smp else [("gate_bc_p", 0), ("gate_bc_p", 1)]
        rc = cnt["xc"] % 2
        cnt["xc"] += 1
        ro = cnt["o"] % NOB
        cnt["o"] += 1
        op("sp", lambda e: e.dma_start(out=xc[rc][0:P, :], in_=src), w=[("xc", rc)], dma=True)
        b0 = nbank()
        b1 = nbank()

        def fn(pe):
            for k in range(8):
                pe.matmul(b0.t[0:P, :], lhsT=B.yT[:, k, i * 128:i * 128 + P], rhs=wout_bf[:, k, 0:512],
                          start=(k == 0), stop=(k == 7))
                ii = pe.matmul(b1.t[0:P, :], lhsT=B.yT[:, k, i * 128:i * 128 + P], rhs=wout_bf[:, k, 512:1024],
                               start=(k == 0), stop=(k == 7))
            return ii
        op("pe", fn, r=[("yT", B.ykey, k) for k in range(8)] + [("wout", 0), ("wout", 1)], w=[b0.res, b1.res])
        o = ob[ro]
        ores = ("o", ro)

        def fn(e):
            e.tensor_tensor(out=o[0:P, 0:512], in0=b0.t[0:P, :], in1=gbc[0:P, 0:512], op=ALU.mult)
            return e.tensor_tensor(out=o[0:P, 512:1024], in0=b1.t[0:P, :], in1=gbc[0:P, 512:1024], op=ALU.mult)
        op("dve", fn, r=[b0.res, b1.res] + gres, w=[ores])
        op("pool", lambda e: e.tensor_tensor(out=o[0:P, :], in0=o[0:P, :], in1=xc[rc][0:P, :], op=ALU.add),
           r=[ores, ("xc", rc)], w=[ores])
        sidx = nstat(2)
        ss = stat[0:P, sidx:sidx + 1]
        rs = stat[0:P, sidx + 1:sidx + 2]
        op("act", lambda e: e.activation(out=junk[0:P, :], in_=o[0:P, :], func=AF.Square, accum_out=ss),
           r=[ores], w=[("stat", sidx), "junk"])
        c_state[(B.idx, ti)] = (P, o, ores, sidx, ss, rs, dst)

    def C_tile2(B, ti):
        P, o, ores, sidx, ss, rs, dst = c_state[(B.idx, ti)]
        op("dve", lambda e: e.tensor_scalar(out=rs, in0=ss, scalar1=1.0 / D, scalar2=EPS, op0=ALU.mult, op1=ALU.add),
           r=[("stat", sidx)], w=[("stat", sidx + 1)])
        op("pool", lambda e: e.tensor_tensor(out=rs, in0=rs, in1=mhalf[0:P, 0:1], op=ALU.pow),
           r=[("stat", sidx + 1), "mhalf"], w=[("stat", sidx + 1)])
        op("dve", lambda e: e.scalar_tensor_tensor(out=o[0:P, :], in0=o[0:P, :], scalar=rs, in1=gf_bc[0:P, :],
                                                   op0=ALU.mult, op1=ALU.mult),
           r=[ores, ("stat", sidx + 1), "gf_bc"], w=[ores])
        ores_out = ("out_y", B.idx, ti)
        op("pool", lambda e: e.dma_start(out=dst, in_=o[0:P, :]), r=[ores], w=[ores_out], dma=True)
        out_res.append(ores_out)

    def C_tile2_pair(B, t0, t1):
        P0, o0, ores0, sidx0, ss0, rs0, dst0 = c_state[(B.idx, t0)]
        P1, o1, ores1, sidx1, ss1, rs1, dst1 = c_state[(B.idx, t1)]
        if sidx1 != sidx0 + 2 or P0 != P1:
            C_tile2(B, t0)
            C_tile2(B, t1)
            return
        P = P0
        v = stat[0:P, sidx0:sidx0 + 4].rearrange("p (t c) -> p t c", c=2)
        ssp, rsp = v[:, :, 0], v[:, :, 1]
        ssres = [("stat", sidx0), ("stat", sidx0 + 2)]
        rsres = [("stat", sidx0 + 1), ("stat", sidx0 + 3)]
        op("dve", lambda e: e.tensor_scalar(out=rsp, in0=ssp, scalar1=1.0 / D, scalar2=EPS, op0=ALU.mult, op1=ALU.add),
           r=ssres, w=rsres)
        op("pool", lambda e: e.tensor_tensor(out=rsp, in0=rsp, in1=mhalf[0:P, 0:2], op=ALU.pow),
           r=rsres + ["mhalf"], w=rsres)
        for (ti, o, ores, rs, dst) in ((t0, o0, ores0, rs0, dst0), (t1, o1, ores1, rs1, dst1)):
            op("dve", lambda e, o=o, rs=rs: e.scalar_tensor_tensor(out=o[0:P, :], in0=o[0:P, :], scalar=rs,
                                                                   in1=gf_bc[0:P, :], op0=ALU.mult, op1=ALU.mult),
               r=[ores] + rsres + ["gf_bc"], w=[ores])
            ores_out = ("out_y", B.idx, ti)
            op("pool", lambda e, o=o, dst=dst: e.dma_start(out=dst, in_=o[0:P, :]), r=[ores], w=[ores_out], dma=True)
            out_res.append(ores_out)

    RIDE = 1
    A_pre(PB[0], 0)
    A_pre(PB[0], 1)
    small_loads_a()
    ada_pe(0)
    wada_dma(3)
    ada_pe(1)
    for m in m_order[:8]:
        win_dma(m)
    ada_pe(2)
    ada_pe(3)
    modT_ops()
    small_loads_b()
    A_post(PB[0], 0)
    A_pre(PB[0], 2)
    A_post(PB[0], 1)
    A_pre(PB[0], 3)
    A_post(PB[0], 2)
    A_post(PB[0], 3)
    for m in m_order[8:12]:
        win_dma(m)
    wada_dma(4)
    wada_dma(5)
    A_pre(PB[1], 0)
    startup_rest()
    B_vpath(PB[0])
    late_dma = [("win", m) for m in m_order[12:]] + [("wout", 0), ("wout", 1)]

    def issue_late(n):
        for _ in range(n):
            if late_dma:
                kind, m = late_dma.pop(0)
                if kind == "win":
                    win_dma(m)
                else:
                    wout_dma(m)
    for b in range(4):
        B = PB[b]
        NBk = PB[b + 1] if b + 1 < 4 else None
        rider = SB if b == RIDE else None
        for j in range(4):
            issue_late(5)
            B_groupA1(B, j)
            if b == RIDE - 1 and j == 2:
                A_post(SB, 0)
            if NBk is not None and j + 1 < 4:
                A_pre(NBk, j + 1)
            if NBk is not None:
                A_post(NBk, j)
            B_groupA2(B, j)
            if b == 0 and j == 0:
                gate_stage1()
            if b == 0 and j == 1:
                gate_stage2()
                bias2_ops()
            if b == 0 and j == 2:
                gate_stage3()
            if b == RIDE - 1 and j == 1:
                A_pre(SB, 0)
                cnt["xa"] += 1
            if b == RIDE - 1 and j == 3:
                B_vpath(SB)
            if rider is not None:
                B_groupA1(rider, j)
                B_groupA2(rider, j)
        B_newconv(B)
        if rider is not None:
            B_newconv(rider)
        for j in range(4):
            B_groupB(B, j)
            if rider is not None:
                B_groupB(rider, j)
            if j == 2 and b + 2 < 4:
                A_pre(PB[b + 2], 0)
        if NBk is not None:
            B_vpath(NBk)
        C_tile(B, 0)
        C_tile(B, 1)
        C_tile(B, 2)
        C_tile2_pair(B, 0, 1)
        C_tile(B, 3)
        if rider is not None:
            C_tile(rider, 0)
        C_tile2_pair(B, 2, 3)
        if rider is not None:
            C_tile2(rider, 0)

    op("sp", lambda e: e.nop(), r=out_res)
    stats = S.emit()
    return nc, stats


_CACHE = {}


def kernel(x_prompt, x_sample, state_conv, c_prompt, c_sample, w_ada, b_ada, g_norm, w_in, conv_w,
           g_v, b_v, w_s, b_s, w_out, g_final):
    if "nc" not in _CACHE:
        _CACHE["nc"] = build_nc()[0]
    nc = _CACHE["nc"]
    f = lambda a: np.ascontiguousarray(np.asarray(a, dtype=np.float32))
    x_prompt, x_sample, state_conv, c_prompt, c_sample = map(f, (x_prompt, x_sample, state_conv, c_prompt, c_sample))
    shared = {
        "w_ada": f(w_ada)[0], "b_ada": f(b_ada)[0], "g_norm": f(g_norm)[0], "w_in": f(w_in)[0],
        "conv_w": f(conv_w)[0], "g_v": f(g_v)[0], "b_v": f(b_v)[0], "w_s": f(w_s)[0], "b_s": f(b_s)[0],
        "w_out": f(w_out)[0], "g_final": f(g_final),
    }
    in_maps = []
    for i in range(NCORES):
        m = dict(shared)
        m["x_p"] = x_prompt[i]
        m["x_s"] = np.ascontiguousarray(x_sample[16 * i:16 * i + 16].reshape(NSAMP, D))
        m["state"] = np.ascontiguousarray(state_conv[0, 16 * i:16 * i + 16].reshape(32, 512))
        m["c_all"] = np.ascontiguousarray(np.concatenate([c_prompt[i:i + 1], c_sample[16 * i:16 * i + 16]], axis=0))
        in_maps.append(m)
    res = run_bass_kernel_spmd(nc, in_maps, core_ids=list(range(NCORES)))
    rs = res.results
    y_prompt = np.stack([rs[i]["y_p"] for i in range(NCORES)], axis=0)
    y_sample = np.concatenate([rs[i]["y_s"].reshape(16, 4, D) for i in range(NCORES)], axis=0)
    new_conv_prompt = np.stack([rs[i]["ncp"] for i in range(NCORES)], axis=0)[None]
    new_conv_sample = np.concatenate([rs[i]["ncs"].reshape(16, 2, 512) for i in range(NCORES)], axis=0)[None]
    new_v_sample = np.concatenate([rs[i]["nvs"].reshape(16, 4, 512) for i in range(NCORES)], axis=0)[None]
    return (y_prompt.astype(np.float32), y_sample.astype(np.float32), new_conv_prompt.astype(np.float32),
            new_conv_sample.astype(np.float32), new_v_sample.astype(np.float32))
```

```python
import numpy as np
import concourse.bass as bass
import concourse.mybir as mybir
from concourse.bass_utils import run_bass_kernel_spmd

F32 = mybir.dt.float32
BF16 = mybir.dt.bfloat16
ALU = mybir.AluOpType
AF = mybir.ActivationFunctionType

EPS = 1e-6
NCORES = 8
D = 1024
SEQ = 2048
NSAMP = 64
PROJ = 3584


class Op:
    __slots__ = ("eng", "fn", "deps", "is_dma", "needed", "sem", "val", "prev")

    def __init__(self, eng, fn, is_dma):
        self.eng = eng
        self.fn = fn
        self.is_dma = is_dma
        self.needed = False
        self.deps = []
        self.sem = None
        self.val = 0
        self.prev = None


class Sched:
    COMPUTE = ("pe", "act", "dve", "pool")
    QUEUES = ("sp", "act", "pool")
    ALL = ("pe", "act", "dve", "pool", "sp")

    def __init__(self, nc, ndma, strict=True):
        self.nc = nc
        self.streams = {e: [] for e in self.ALL}
        self.last_w = {}
        self.readers = {}
        self.ndma = ndma
        self.strict = strict

    def op(self, eng, fn, reads=(), writes=(), dma=False):
        o = Op(eng, fn, dma)
        self.nops = getattr(self, "nops", 0) + 1
        if self.nops > getattr(self, "limit", 10 ** 9):
            return o
        deps = {}
        for r in reads:
            w = self.last_w.get(r)
            if w is not None:
                deps[id(w)] = (w, "raw")
        for wr in writes:
            w = self.last_w.get(wr)
            if w is not None and id(w) not in deps:
                deps[id(w)] = (w, "waw")
            for rd in self.readers.get(wr, ()):
                if id(rd) not in deps:
                    deps[id(rd)] = (rd, "war")
        for d, kind in deps.values():
            if d is o:
                continue
            if (not d.is_dma) and (not dma) and d.eng == eng:
                if eng == "pe":
                    continue
                if kind != "raw" and not self.strict:
                    continue
            o.deps.append(d)
            d.needed = True
        for r in reads:
            self.readers.setdefault(r, []).append(o)
        for wr in writes:
            self.last_w[wr] = o
            self.readers[wr] = []
        self.streams[eng].append(o)
        return o

    def alias(self, new_res, old_res_list):
        rd = []
        for r in old_res_list:
            w = self.last_w.get(r)
            if w is not None:
                rd.append(w)
            rd.extend(self.readers.get(r, ()))
        self.readers.setdefault(new_res, []).extend(rd)

    def emit(self):
        nc = self.nc
        sems = {e: nc.alloc_semaphore(name=f"sem_{e}") for e in self.COMPUTE}
        dsems = {q: [nc.alloc_semaphore(name=f"dsem_{q}_{i}") for i in range(self.ndma[q])]
                 for q in self.QUEUES}
        for e in self.ALL:
            c = 0
            dmas = []
            for o in self.streams[e]:
                if o.is_dma:
                    n = self.ndma[e]
                    di = len(dmas)
                    o.sem = dsems[e][di % n]
                    o.val = 16 * (di // n + 1)
                    o.prev = dmas[di - n] if di >= n else None
                    dmas.append(o)
                elif o.needed:
                    c += 1
                    o.sem = sems[e]
                    o.val = c
        streams = self.streams
        stats = {}

        def run(e, eng):
            seen = {}
            nw = 0
            for o in streams[e]:
                need = {}
                waits = list(o.deps)
                if o.is_dma and o.prev is not None:
                    waits.append(o.prev)
                for d in waits:
                    key = d.sem.num
                    if key not in need or need[key][1] < d.val:
                        need[key] = (d.sem, d.val)
                for key, (sem, val) in need.items():
                    if seen.get(key, 0) >= val:
                        continue
                    eng.wait_ge(sem, val)
                    seen[key] = val
                    nw += 1
                ins = o.fn(eng)
                if o.is_dma:
                    ins.then_inc(o.sem, 16)
                elif o.needed:
                    ins.then_inc(o.sem, 1)
            stats[e] = (len(streams[e]), nw)

        with nc.Block() as block:
            @block.tensor
            def _(eng):
                run("pe", eng)

            @block.scalar
            def _(eng):
                run("act", eng)

            @block.vector
            def _(eng):
                run("dve", eng)

            @block.gpsimd
            def _(eng):
                run("pool", eng)

            @block.sync
            def _(eng):
                run("sp", eng)
                for q in self.QUEUES:
                    last = {}
                    for o in streams[q]:
                        if o.is_dma:
                            last[o.sem.num] = (o.sem, o.val)
                    for sem, val in last.values():
                        eng.wait_ge(sem, val)
        return stats


class Alloc:
    def __init__(self, nc):
        self.nc = nc
        self.off = (nc.sbuf_base + 63) // 64 * 64
        self.top = nc.sbuf_top
        self.hi = self.off

    def __call__(self, name, shape, dtype):
        isz = 2 if dtype == BF16 else 4
        size = isz
        for s in shape[1:]:
            size *= s
        off = self.off
        self.off += (size + 63) // 64 * 64
        self.hi = max(self.hi, self.off)
        assert self.off <= self.top, f"SBUF overflow at {name}: {self.off} > {self.top}"
        return self.nc.alloc_sbuf_tensor_at(name, list(shape), dtype, offset=off)


class Bank:
    def __init__(self, t, res):
        self.t = t
        self.res = res


def build_nc(limit=None):
    nc = bass.Bass("TRN2", target_bir_lowering=False)

    def din(name, shape):
        return nc.dram_tensor(name, list(shape), F32, kind="ExternalInput").ap()

    def dout(name, shape):
        return nc.dram_tensor(name, list(shape), F32, kind="ExternalOutput").ap()

    x_p = din("x_p", [SEQ, D])
    x_s = din("x_s", [NSAMP, D])
    state = din("state", [32, 512])
    c_all = din("c_all", [17, D])
    w_ada = din("w_ada", [D, 3 * D])
    b_ada = din("b_ada", [3 * D])
    g_norm = din("g_norm", [D])
    w_in = din("w_in", [D, PROJ])
    conv_w = din("conv_w", [3, 512])
    g_v = din("g_v", [512])
    b_v = din("b_v", [512])
    w_s = din("w_s", [8, 128, 128])
    b_s = din("b_s", [8, 128])
    w_out = din("w_out", [D, D])
    g_final = din("g_final", [D])
    y_p = dout("y_p", [SEQ, D])
    y_s = dout("y_s", [NSAMP, D])
    ncp = dout("ncp", [2, 512])
    ncs = dout("ncs", [32, 512])
    nvs = dout("nvs", [NSAMP, 512])

    S = Sched(nc, ndma={"sp": 8, "act": 20, "pool": 40}, strict=True)
    if limit is not None:
        S.limit = limit
    A = Alloc(nc)

    def op(eng, fn, r=(), w=(), dma=False):
        return S.op(eng, fn, reads=r, writes=w, dma=dma)

    win_bf = A("win_bf", [128, 8, PROJ], BF16)
    wout_bf = A("wout_bf", [128, 8, D], BF16)
    hT = [A(f"hT{i}", [128, 8, 512], BF16) for i in range(2)]
    xa = [A(f"xa{i}", [128, D], F32) for i in range(2)]
    xn = [A(f"xn{i}", [128, D], BF16) for i in range(2)]
    junk = A("junk", [128, D], BF16)
    spad = [A(f"spad{j}", [128, 514], F32) for j in range(4)]
    ssamp = [A(f"ssamp{j}", [128, 16, 6], F32) for j in range(4)]
    gv_bc = A("gv_bc", [128, 512], F32)
    bv_bc = A("bv_bc", [128, 512], F32)
    gf_bc = A("gf_bc", [128, D], F32)
    gate_bc_p = A("gate_bc_p", [128, D], F32)
    gate_bc_s = A("gate_bc_s", [128, D], F32)
    bsT = A("bsT", [128, 4, 128], F32)
    Wt = A("Wt", [128, 8, 128], BF16)
    Wblk = A("Wblk", [64, 8, 64], BF16)
    ident_bf = A("ident_bf", [128, 128], BF16)
    ident_f = A("ident_f", [128, 128], F32)
    modT = A("modT", [128, 24, 17], F32)
    aT = A("aT", [128, 8, 17], F32)
    gT = A("gT", [128, 8], F32)
    badaT = A("badaT", [128, 24], F32)
    cwT = A("cwT", [128, 4, 3], F32)
    selP = A("selP", [17, 128], F32)
    selS = A("selS", [17, 64], F32)
    stat = A("stat", [128, 64], F32)
    mhalf = A("mhalf", [128, 4], F32)
    st6 = [A(f"st6_{i}", [128, 6], F32) for i in range(4)]
    ncs_sb = A("ncs_sb", [32, 512], F32)
    ncp_sb = ncs_sb
    nsT = A("nsT", [128, 4, 32], F32)
    vnS = A("vnS", [64, 512], F32)
    gvT = A("gvT", [128, 4], F32)
    bvT = A("bvT", [128, 4], F32)
    bias2 = A("bias2", [128, 4, 128], F32)
    ones_bf = A("ones_bf", [128, 64], BF16)
    hT_s = A("hT_s", [128, 8, 64], BF16)
    yT_s = A("yT_s", [128, 8, 64], BF16)
    vn_s = A("vn_s", [64, 512], BF16)
    NTMPS = 6
    tmps = [A(f"tmps{i}", [128, 64], F32) for i in range(NTMPS)]

    silucT = A("silucT", [128, 8, 17], BF16)
    mark = A.off
    c_sb = A("c_sb", [17, D], F32)
    sc = A("sc", [17, D], F32)
    ws_nat = A("ws_nat", [128, 8, 128], F32)
    wsT_f = A("wsT_f", [128, 8, 128], F32)
    state_sb = A("state_sb", [32, 512], F32)
    wada0 = A("wada0", [128, 8, 512], BF16)
    late_mark = A.off
    gate_rows = A("gate_rows", [17, D], F32)
    wada1 = A("wada1", [128, 8, 512], BF16)
    wada2 = A("wada2", [128, 8, 512], BF16)
    wada = [wada0, wada1, wada2]
    startup_end = A.off
    early_startup_res = ["c_sb", "sc", "ws_nat", ("wsT_f", 0), ("wsT_f", 1), "wsT_m", "state_sb", ("wada", 0)]
    late_startup_res = [("gate_rows", 0), ("gate_rows", 1), ("wada", 1), ("wada", 2), "silucT"]
    A.off = mark
    NTMP = 9
    NOB = 3
    NVZ = 4
    tmpb = [A(f"tmp{i}", [128, 512], F32) for i in range(NTMP)]
    vz = [A(f"vz{i}", [128, 512], F32) for i in range(NVZ)]
    assert A.off <= late_mark, (A.off, late_mark)
    A.off = late_mark
    xc = [A(f"xc{i}", [128, D], F32) for i in range(2)]
    ob = [A(f"ob{i}", [128, D], F32) for i in range(NOB)]
    A.off = max(A.off, startup_end)
    yT = A("yT", [128, 8, 512], BF16)
    vn_bf = [A(f"vn_bf{i}", [128, 512], BF16) for i in range(4)]
    early_main = [("tmp", i) for i in range(NTMP)] + [("vz", i) for i in range(NVZ)]
    late_main = [("xc", i) for i in range(2)] + [("o", i) for i in range(NOB)]

    pT = [nc.alloc_psum_tensor(f"pT{i}", [128, 8, 128], BF16) for i in range(2)]
    NRING = 6
    pbank = [Bank(nc.alloc_psum_tensor(f"pb{i}", [128, 512], F32), ("bank", i)) for i in range(NRING)]
    ring_i = [0]

    def nbank():
        b = pbank[ring_i[0] % NRING]
        ring_i[0] += 1
        return b

    tmp_i = [0]

    def ntmp():
        i = tmp_i[0] % NTMP
        tmp_i[0] += 1
        return tmpb[i], ("tmp", i)

    op("pool", lambda e: e.memset(ident_f[:], 0.0), w=["ident_f"])
    op("pool", lambda e: e.affine_select(out=ident_f[:], in_=ident_f[:], compare_op=ALU.not_equal,
                                         fill=1.0, base=0, pattern=[[-1, 128]], channel_multiplier=1),
       r=["ident_f"], w=["ident_f"])
    op("pool", lambda e: e.tensor_copy(out=ident_bf[:], in_=ident_f[:]), r=["ident_f"], w=["ident_bf"])
    op("pool", lambda e: e.memset(selP[:], 0.0), w=["selP"])
    op("pool", lambda e: e.memset(selP[0:1, :], 1.0), r=["selP"], w=["selP"])
    op("pool", lambda e: e.memset(selS[:], 1.0), w=["selS"])
    op("pool", lambda e: e.affine_select(out=selS[:], in_=selS[:], compare_op=ALU.is_ge, fill=0.0,
                                         base=4, pattern=[[1, 64]], channel_multiplier=-4),
       r=["selS"], w=["selS"])
    op("pool", lambda e: e.affine_select(out=selS[:], in_=selS[:], compare_op=ALU.is_ge, fill=0.0,
                                         base=-1, pattern=[[-1, 64]], channel_multiplier=4),
       r=["selS"], w=["selS"])
    for j in range(4):
        op("pool", lambda e, j=j: e.memset(spad[j][:, 0:2], 0.0), w=[("spadH", j)])
    op("pool", lambda e: e.memset(Wblk[:], 0.0), w=["Wblk"])
    op("pool", lambda e: e.memset(mhalf[:], -0.5), w=["mhalf"])
    op("pool", lambda e: e.memset(ones_bf[:], 1.0), w=["ones_bf"])

    def sp_load(dst, src, res, nonc=False, q="act"):
        if nonc:
            op(q, lambda e: e.dma_start(out=dst, in_=src, allow_slow_non_contiguous=True), w=[res], dma=True)
        else:
            op(q, lambda e: e.dma_start(out=dst, in_=src), w=[res], dma=True)

    sp_load(c_sb[:], c_all[:, :], "c_sb", q="sp")
    def small_loads_a():
        sp_load(gT[:], g_norm.rearrange("(k p) -> p k", p=128), "gT", nonc=True)
        sp_load(badaT[:], b_ada.rearrange("(c p) -> p c", p=128), "badaT", nonc=True)
        for k3 in range(3):
            sp_load(cwT[:, :, k3], conv_w[k3, :].rearrange("(j p) -> p j", p=128), ("cwT", k3), nonc=True)
    cw_res = [("cwT", k3) for k3 in range(3)]

    w_ada_v = w_ada.rearrange("(k p) n -> p k n", p=128)
    w_in_v = w_in.rearrange("(k p) n -> p k n", p=128)
    w_out_v = w_out.rearrange("(k p) n -> p k n", p=128)

    def wada_dma(blk):
        slot = blk % 3
        op("pool", lambda e: e.dma_start(out=wada[slot][:], in_=w_ada_v[:, :, blk * 512:(blk + 1) * 512]),
           w=[("wada", slot)], dma=True)

    for blk in range(3):
        wada_dma(blk)

    def small_loads_b():
        sp_load(gvT[:], g_v.rearrange("(j p) -> p j", p=128), "gvT", nonc=True)
        sp_load(bvT[:], b_v.rearrange("(j p) -> p j", p=128), "bvT", nonc=True)
        sp_load(gv_bc[:], g_v.partition_broadcast(128), "gv_bc")
        sp_load(bv_bc[:], b_v.partition_broadcast(128), "bv_bc")
        sp_load(gf_bc[:], g_final.partition_broadcast(128), "gf_bc")
        for h in range(8):
            sp_load(bsT[64 * (h % 2):64 * (h % 2) + 64, h // 2, :], b_s[h, :].partition_broadcast(64), ("bsT", h))
        sp_load(ws_nat[:], w_s.rearrange("h t s -> t h s"), "ws_nat")
        sp_load(state_sb[:], state[:, :], "state_sb")

    op("act", lambda e: e.activation(out=sc[:], in_=c_sb[:], func=AF.Silu), r=["c_sb"], w=["sc"])
    bk = nbank()

    def fn(pe, bk=bk):
        for k in range(8):
            i = pe.transpose(out=bk.t[:, k * 17:(k + 1) * 17], in_=sc[:, k * 128:(k + 1) * 128],
                             identity=ident_f[0:17, 0:17])
        return i
    op("pe", fn, r=["sc", "ident_f"], w=[bk.res])
    op("dve", lambda e, bk=bk: e.tensor_copy(out=silucT[:], in_=bk.t[:, 0:136].rearrange("p (k j) -> p k j", j=17)),
       r=[bk.res], w=["silucT"])

    modps = nbank()
    modps2 = nbank()

    def ada_pe(blk):
        slot = blk % 3
        tgt = modps if blk < 4 else modps2

        def fn(pe):
            for c4 in range(4):
                c = blk * 4 + c4
                cc = c if blk < 4 else c - 16
                for k in range(8):
                    i = pe.matmul(tgt.t[:, cc * 17:(cc + 1) * 17], lhsT=wada[slot][:, k, c4 * 128:(c4 + 1) * 128],
                                  rhs=silucT[:, k, :], start=(k == 0), stop=(k == 7))
            return i
        op("pe", fn, r=[("wada", slot), "silucT"], w=[(tgt.res, blk)] if False else [tgt.res])

    m_order = [20, 21, 22, 23]
    for j in range(4):
        m_order += [j, 8 + j, 4 + j, 12 + j]
    for j in range(4):
        m_order += [16 + j, 24 + j]

    def win_dma(m):
        op("pool", lambda e: e.dma_start(out=win_bf[:, :, m * 128:(m + 1) * 128], in_=w_in_v[:, :, m * 128:(m + 1) * 128]),
           w=[("win", m)], dma=True)

    def modT_ops():
        op("dve", lambda e: e.tensor_tensor(out=modT[:, 0:16, :],
                                            in0=modps.t[:, 0:272].rearrange("p (c j) -> p c j", j=17),
                                            in1=badaT[:, 0:16].unsqueeze(2).to_broadcast([128, 16, 17]), op=ALU.add),
           r=[modps.res, "badaT"], w=["modT_ss"])
        op("dve", lambda e: e.scalar_tensor_tensor(out=aT[:], in0=modT[:, 8:16, :], scalar=1.0,
                                                   in1=gT[:].unsqueeze(2).to_broadcast([128, 8, 17]),
                                                   op0=ALU.add, op1=ALU.mult),
           r=["modT_ss", "gT"], w=["aT"])

    def wout_dma(h2):
        op("pool", lambda e: e.dma_start(out=wout_bf[:, :, h2 * 512:(h2 + 1) * 512],
                                         in_=w_out_v[:, :, h2 * 512:(h2 + 1) * 512]),
           w=[("wout", h2)], dma=True)

    wblk_res = [("Wblk", j) for j in range(16)]

    def startup_rest():
        for half in range(2):
            bk = nbank()

            def fn(pe, bk=bk, half=half):
                for hh in range(4):
                    i = pe.transpose(out=bk.t[:, hh * 128:(hh + 1) * 128], in_=ws_nat[:, half * 4 + hh, :],
                                     identity=ident_f[:])
                return i
            op("pe", fn, r=["ws_nat", "ident_f"], w=[bk.res])
            op("act", lambda e, bk=bk, half=half: e.copy(out=wsT_f[:, half * 4:(half + 1) * 4, :],
                                                         in_=bk.t[:].rearrange("p (h t) -> p h t", t=128)),
               r=[bk.res], w=[("wsT_f", half)])
        op("pool", lambda e: e.affine_select(out=wsT_f[:], in_=wsT_f[:], compare_op=ALU.is_ge, fill=0.0, base=0,
                                             pattern=[[0, 8], [1, 128]], channel_multiplier=-1),
           r=[("wsT_f", 0), ("wsT_f", 1)], w=["wsT_m"])
        op("pool", lambda e: e.tensor_copy(out=Wt[:], in_=wsT_f[:]), r=["wsT_m"], w=["Wt"])
        for j in range(16):
            op("sp", lambda e, j=j: e.dma_start(out=Wblk[4 * j:4 * j + 4, :, 4 * j:4 * j + 4], in_=Wt[0:4, :, 0:4],
                                                 allow_slow_non_contiguous=True),
               r=["Wt", "Wblk"], w=[("Wblk", j)], dma=True)

        bk = nbank()

        def fn(pe, bk=bk):
            for j in range(4):
                i = pe.transpose(out=bk.t[:, j * 32:(j + 1) * 32], in_=state_sb[0:32, j * 128:(j + 1) * 128],
                                 identity=ident_f[0:32, 0:32])
            return i
        op("pe", fn, r=["state_sb", "ident_f"], w=[bk.res])

        def fn(e, bk=bk):
            for j in range(4):
                i = e.tensor_copy(out=ssamp[j][:, :, 0:2],
                                  in_=bk.t[:, j * 32:(j + 1) * 32].rearrange("p (s r) -> p s r", r=2))
            return i
        op("dve", fn, r=[bk.res], w=[("ssampH", j) for j in range(4)])

        for rname in early_main:
            S.alias(rname, early_startup_res)

    def bias2_ops():
        bk = nbank()

        def fn(pe):
            for h in range(8):
                ii = pe.matmul(bk.t[64 * (h % 2):64 * (h % 2) + 64, (h // 2) * 128:(h // 2 + 1) * 128],
                               lhsT=ones_bf[:, 0:64], rhs=Wt[:, h, :], start=True, stop=True)
            return ii
        op("pe", fn, r=["Wt", "ones_bf"], w=[bk.res])

        def fd(e):
            for jj in range(4):
                ii = e.scalar_tensor_tensor(out=bias2[:, jj, :], in0=bk.t[:, jj * 128:(jj + 1) * 128],
                                            scalar=bvT[:, jj:jj + 1], in1=bsT[:, jj, :], op0=ALU.mult, op1=ALU.add)
            return ii
        op("dve", fd, r=[bk.res, "bvT"] + [("bsT", h) for h in range(8)], w=["bias2"])

    gate_state = {}

    def gate_stage1():
        ada_pe(4)
        ada_pe(5)
        op("dve", lambda e: e.tensor_tensor(out=modT[:, 16:24, :],
                                            in0=modps2.t[:, 0:136].rearrange("p (c j) -> p c j", j=17),
                                            in1=badaT[:, 16:24].unsqueeze(2).to_broadcast([128, 8, 17]), op=ALU.add),
           r=[modps2.res, "badaT"], w=["modT_g"])

    def gate_stage2():
        gbk = [nbank(), nbank()]
        for half in range(2):
            def fn(pe, half=half):
                for kk in range(4):
                    i = pe.transpose(out=gbk[half].t[0:17, kk * 128:(kk + 1) * 128], in_=modT[:, 16 + half * 4 + kk, :],
                                     identity=ident_f[:])
                return i
            op("pe", fn, r=["modT_g", "ident_f"], w=[gbk[half].res])
            op("act", lambda e, half=half: e.copy(out=gate_rows[:, half * 512:(half + 1) * 512], in_=gbk[half].t[0:17, :]),
               r=[gbk[half].res], w=[("gate_rows", half)])

    def gate_stage3():
        for half in range(2):
            bk = nbank()
            op("pe", lambda pe, bk=bk, half=half: pe.matmul(bk.t[:, :], lhsT=selP[:, :],
                                                           rhs=gate_rows[:, half * 512:(half + 1) * 512],
                                                           start=True, stop=True),
               r=["selP", ("gate_rows", half)], w=[bk.res])
            op("act", lambda e, bk=bk, half=half: e.copy(out=gate_bc_p[:, half * 512:(half + 1) * 512], in_=bk.t[:, :]),
               r=[bk.res], w=[("gate_bc_p", half)])
            bk = nbank()
            op("pe", lambda pe, bk=bk, half=half: pe.matmul(bk.t[0:64, :], lhsT=selS[:, :],
                                                           rhs=gate_rows[:, half * 512:(half + 1) * 512],
                                                           start=True, stop=True),
               r=["selS", ("gate_rows", half)], w=[bk.res])
            op("act", lambda e, bk=bk, half=half: e.copy(out=gate_bc_s[0:64, half * 512:(half + 1) * 512], in_=bk.t[0:64, :]),
               r=[bk.res], w=[("gate_bc_s", half)])
        for rname in late_main:
            S.alias(rname, late_startup_res)

    class Blk:
        pass

    def mkblk(b):
        B = Blk()
        B.idx = b
        if b < 4:
            B.N, B.tiles, B.t0, B.sample = 512, [(i, 128) for i in range(4)], b * 512, False
            B.hT, B.hkey, B.yT, B.ykey, B.vn, B.vkey = hT[b % 2], b % 2, yT, "p", vn_bf, "p"
        else:
            B.N, B.tiles, B.t0, B.sample = 64, [(0, 64)], 0, True
            B.hT, B.hkey, B.yT, B.ykey, B.vn, B.vkey = hT_s, "s", yT_s, "s", [vn_s], "s"
        return B

    PB = [mkblk(b) for b in range(4)]
    SB = mkblk(4)

    cnt = {"xa": 0, "pT": 0, "xc": 0, "o": 0, "vz": 0, "st": 0}
    st_i = [0]

    def nstat(n=1):
        i = st_i[0] % 60
        if i + n > 60:
            i = 0
        st_i[0] = i + n
        return i

    tmps_i = [0]

    def ntmp_for(B):
        if not B.sample:
            return ntmp()
        i = tmps_i[0] % NTMPS
        tmps_i[0] += 1
        return tmps[i], ("tmps", i)

    out_res = []
    a_state = {}

    def A_pre(B, ti):
        i, P = B.tiles[ti]
        src = x_s[0:64, :] if B.sample else x_p[B.t0 + i * 128: B.t0 + (i + 1) * 128, :]
        ra = cnt["xa"] % 2
        cnt["xa"] += 1
        sidx = nstat(2)
        ss = stat[0:P, sidx:sidx + 1]
        rs = stat[0:P, sidx + 1:sidx + 2]
        op("sp", lambda e: e.dma_start(out=xa[ra][0:P, :], in_=src), w=[("xa", ra)], dma=True)
        op("act", lambda e: e.activation(out=junk[0:P, :], in_=xa[ra][0:P, :], func=AF.Square, accum_out=ss),
           r=[("xa", ra)], w=[("stat", sidx), "junk"])
        op("dve", lambda e: e.tensor_scalar(out=rs, in0=ss, scalar1=1.0 / D, scalar2=EPS, op0=ALU.mult, op1=ALU.add),
           r=[("stat", sidx)], w=[("stat", sidx + 1)])
        op("pool", lambda e: e.tensor_tensor(out=rs, in0=rs, in1=mhalf[0:P, 0:1], op=ALU.pow),
           r=[("stat", sidx + 1), "mhalf"], w=[("stat", sidx + 1)])
        op("dve", lambda e: e.tensor_scalar(out=xn[ra][0:P, :], in0=xa[ra][0:P, :], scalar1=rs, scalar2=None,
                                            op0=ALU.mult),
           r=[("xa", ra), ("stat", sidx + 1)], w=[("xn", ra)])
        a_state[(B.idx, ti)] = ra

    def A_post(B, ti):
        i, P = B.tiles[ti]
        ra = a_state[(B.idx, ti)]
        pa = cnt["pT"] % 2
        cnt["pT"] += 1
        hTb = B.hT

        def fn(pe):
            for k in range(8):
                ii = pe.transpose(out=pT[pa][:, k, 0:P], in_=xn[ra][0:P, k * 128:(k + 1) * 128],
                                  identity=ident_bf[0:P, 0:P])
            return ii
        op("pe", fn, r=[("xn", ra), "ident_bf"], w=[("pT", pa)])
        if not B.sample:
            def fd(e):
                for k in range(8):
                    ii = e.tensor_scalar(out=hTb[:, k, i * 128:(i + 1) * 128], in0=pT[pa][:, k, :],
                                         scalar1=aT[:, k, 0:1], scalar2=modT[:, k, 0:1], op0=ALU.mult, op1=ALU.add)
                return ii

            def fa(e):
                for k in range(8):
                    ii = e.activation(out=hTb[:, k, i * 128:(i + 1) * 128], in_=pT[pa][:, k, :], func=AF.Identity,
                                      bias=modT[:, k, 0:1], scale=aT[:, k, 0:1])
                return ii
            if ti % 2 == 0:
                op("dve", fd, r=[("pT", pa), "aT", "modT_ss"], w=[("hT", B.hkey, i)])
            else:
                op("act", fa, r=[("pT", pa), "aT", "modT_ss"], w=[("hT", B.hkey, i)])
        else:
            tb, tres = ntmp()
            tv = tb[:, :].rearrange("p (k s i) -> p k s i", k=8, i=4)
            op("dve", lambda e: e.tensor_tensor(out=tv, in0=pT[pa][:, :, 0:64].rearrange("p k (s i) -> p k s i", i=4),
                                                in1=aT[:, :, 1:17].unsqueeze(3).to_broadcast([128, 8, 16, 4]),
                                                op=ALU.mult),
               r=[("pT", pa), "aT"], w=[tres])
            op("dve", lambda e: e.tensor_tensor(out=hTb[:, :, 0:64].rearrange("p k (s i) -> p k s i", i=4), in0=tv,
                                                in1=modT[:, 0:8, 1:17].unsqueeze(3).to_broadcast([128, 8, 16, 4]),
                                                op=ALU.add),
               r=[tres, "modT_ss"], w=[("hT", B.hkey, 0)])

    def hT_res(B):
        return [("hT", B.hkey, i) for (i, P) in B.tiles]

    def mm_proj(B, m):
        N = B.N
        bk = nbank()

        def fn(pe):
            for k in range(8):
                ii = pe.matmul(bk.t[:, 0:N], lhsT=win_bf[:, k, m * 128:(m + 1) * 128], rhs=B.hT[:, k, 0:N],
                               start=(k == 0), stop=(k == 7))
            return ii
        op("pe", fn, r=[("win", m)] + hT_res(B), w=[bk.res])
        return bk

    def B_vpath(B):
        nt = len(B.tiles)
        P = B.tiles[0][1]
        sidx = nstat(16)
        mvall = stat[0:P, sidx:sidx + 8]
        mv3 = mvall.rearrange("p (t c) -> p t c", c=2)
        rstd4 = stat[0:P, sidx + 8:sidx + 8 + nt]
        nmr4 = stat[0:P, sidx + 12:sidx + 12 + nt]
        mvres = [("stat", sidx + c) for c in range(8)]
        rres = [("stat", sidx + 8 + c) for c in range(4)]
        nres = [("stat", sidx + 12 + c) for c in range(4)]
        vis = []
        for idx, (i, P_) in enumerate(B.tiles):
            bk = nbank()

            def fn(pe, bk=bk, i=i):
                for k in range(8):
                    ii = pe.matmul(bk.t[0:P, :], lhsT=B.hT[:, k, i * 128:i * 128 + P], rhs=win_bf[:, k, 2560:3072],
                                   start=(k == 0), stop=(k == 7))
                return ii
            op("pe", fn, r=[("win", m) for m in (20, 21, 22, 23)] + hT_res(B), w=[bk.res])
            s6 = cnt["st"] % 4
            cnt["st"] += 1
            vi = cnt["vz"] % NVZ
            cnt["vz"] += 1
            vis.append(vi)
            op("act", lambda e, bk=bk, vi=vi: e.copy(out=vz[vi][0:P, :], in_=bk.t[0:P, :]),
               r=[bk.res], w=[("vz", vi)])
            op("dve", lambda e, vi=vi, s6=s6: e.bn_stats(out=st6[s6][0:P, :], in_=vz[vi][0:P, :]),
               r=[("vz", vi)], w=[("st6", s6)])
            op("dve", lambda e, s6=s6, idx=idx: e.bn_aggr(out=mvall[:, 2 * idx:2 * idx + 2], in_=st6[s6][0:P, :]),
               r=[("st6", s6)], w=[mvres[2 * idx], mvres[2 * idx + 1]])
        op("dve", lambda e: e.tensor_scalar(out=rstd4, in0=mv3[:, 0:nt, 1], scalar1=EPS, scalar2=None, op0=ALU.add),
           r=mvres[0:2 * nt], w=rres)
        op("pool", lambda e: e.tensor_tensor(out=rstd4, in0=rstd4, in1=mhalf[0:P, 0:nt], op=ALU.pow),
           r=rres + ["mhalf"], w=rres)
        op("dve", lambda e: e.scalar_tensor_tensor(out=nmr4, in0=mv3[:, 0:nt, 0], scalar=-1.0, in1=rstd4,
                                                   op0=ALU.mult, op1=ALU.mult),
           r=mvres[0:2 * nt] + rres, w=nres)
        for idx, (i, P_) in enumerate(B.tiles):
            vi = vis[idx]
            rstd = rstd4[:, idx:idx + 1]
            nmr = nmr4[:, idx:idx + 1]
            if not B.sample:
                op("act", lambda e, vi=vi, i=i, rstd=rstd, nmr=nmr: e.activation(
                    out=vn_bf[i][:, :], in_=vz[vi][:, :], func=AF.Identity, bias=nmr, scale=rstd),
                   r=[("vz", vi)] + rres + nres, w=[("vn", "p", i)])
            else:
                op("act", lambda e, vi=vi, rstd=rstd, nmr=nmr: e.activation(
                    out=vz[vi][0:P, :], in_=vz[vi][0:P, :], func=AF.Identity, bias=nmr, scale=rstd),
                   r=[("vz", vi)] + rres + nres, w=[("vz", vi)])
                op("pool", lambda e, vi=vi: e.tensor_tensor(out=vz[vi][0:P, :], in0=vz[vi][0:P, :],
                                                            in1=gv_bc[0:P, :], op=ALU.mult),
                   r=[("vz", vi), "gv_bc"], w=[("vz", vi)])
                op("pool", lambda e, vi=vi: e.tensor_tensor(out=vnS[0:64, :], in0=vz[vi][0:64, :], in1=bv_bc[0:64, :],
                                                            op=ALU.add),
                   r=[("vz", vi), "bv_bc"], w=["vnS"])
                op("act", lambda e: e.copy(out=vn_s[0:64, :], in_=vnS[0:64, :]), r=["vnS"], w=[("vn", "s", 0)])
                op("pool", lambda e: e.dma_start(out=nvs[:, :], in_=vnS[0:64, :]), r=["vnS"], w=["out_nvs"], dma=True)
                out_res.append("out_nvs")

    ga_state = {}

    def B_groupA1(B, j):
        N = B.N
        smp = B.sample
        bH = mm_proj(B, j)
        bC = mm_proj(B, 8 + j)
        t1, t1r = ntmp_for(B)
        op("act", lambda e: e.copy(out=t1[:, 0:N], in_=bH.t[:, 0:N]), r=[bH.res], w=[t1r])
        if not smp:
            if B.idx > 0:
                op("dve", lambda e: e.tensor_copy(out=spad[j][:, 0:2], in_=spad[j][:, 512:514]),
                   r=[("spadB", j)], w=[("spadH", j)])
            s_body = spad[j][:, 2:514]
            sres_b, sres_h = ("spadB", j), ("spadH", j)
            s0, s1, s2 = spad[j][:, 0:512], spad[j][:, 1:513], spad[j][:, 2:514]
            v3 = lambda ap: ap
        else:
            s_body = ssamp[j][:, :, 2:6]
            sres_b, sres_h = ("ssampB", j), ("ssampH", j)
            s0, s1, s2 = ssamp[j][:, :, 0:4], ssamp[j][:, :, 1:5], ssamp[j][:, :, 2:6]
            v3 = lambda ap: ap.rearrange("p (s i) -> p s i", i=4)
        op("dve", lambda e: e.tensor_tensor(out=s_body, in0=v3(bC.t[:, 0:N]), in1=v3(t1[:, 0:N]), op=ALU.mult),
           r=[bC.res, t1r], w=[sres_b])
        c1, c1r = ntmp_for(B)
        c2, c2r = ntmp_for(B)
        op("act", lambda e: e.activation(out=v3(c1[:, 0:N]), in_=s2, func=AF.Copy, scale=cwT[:, j, 2:3]),
           r=[sres_b] + cw_res, w=[c1r])
        op("dve", lambda e: e.scalar_tensor_tensor(out=v3(c2[:, 0:N]), in0=s1, scalar=cwT[:, j, 1:2], in1=v3(c1[:, 0:N]),
                                                   op0=ALU.mult, op1=ALU.add),
           r=[sres_b, sres_h, c1r] + cw_res, w=[c2r])
        op("dve", lambda e: e.scalar_tensor_tensor(out=v3(c1[:, 0:N]), in0=s0, scalar=cwT[:, j, 0:1], in1=v3(c2[:, 0:N]),
                                                   op0=ALU.mult, op1=ALU.add),
           r=[sres_b, sres_h, c2r] + cw_res, w=[c1r])
        ga_state[(B.idx, j)] = (c1, c1r, c2, c2r)

    def B_groupA2(B, j):
        N = B.N
        c1, c1r, c2, c2r = ga_state[(B.idx, j)]
        bB = mm_proj(B, 4 + j)
        bZ = mm_proj(B, 12 + j)
        sg, sgr = ntmp_for(B)
        op("act", lambda e: e.activation(out=sg[:, 0:N], in_=bZ.t[:, 0:N], func=AF.Silu), r=[bZ.res], w=[sgr])
        op("dve", lambda e: e.tensor_tensor(out=c2[:, 0:N], in0=bB.t[:, 0:N], in1=c1[:, 0:N], op=ALU.mult),
           r=[bB.res, c1r], w=[c2r])
        op("pool", lambda e: e.tensor_tensor(out=B.yT[:, j, 0:N], in0=c2[:, 0:N], in1=sg[:, 0:N], op=ALU.mult),
           r=[c2r, sgr], w=[("yT", B.ykey, j)])

    def B_newconv(B):
        if B.sample:
            def fn(e):
                for j in range(4):
                    ii = e.tensor_copy(out=nsT[:, j, :].rearrange("p (s r) -> p s r", r=2), in_=ssamp[j][:, :, 4:6])
                return ii
            op("dve", fn, r=[("ssampB", j) for j in range(4)], w=["nsT"])
            bk = nbank()

            def fn(pe):
                for j in range(4):
                    ii = pe.transpose(out=bk.t[0:32, j * 128:(j + 1) * 128], in_=nsT[:, j, :], identity=ident_f[:])
                return ii
            op("pe", fn, r=["nsT", "ident_f"], w=[bk.res])
            op("act", lambda e: e.copy(out=ncs_sb[:, :], in_=bk.t[0:32, :]), r=[bk.res], w=["ncs_sb"])
            op("pool", lambda e: e.dma_start(out=ncs[:, :], in_=ncs_sb[:, :]), r=["ncs_sb"], w=["out_ncs"], dma=True)
            out_res.append("out_ncs")
        elif B.idx == 3:
            bk = nbank()

            def fn(pe):
                for j in range(4):
                    ii = pe.transpose(out=bk.t[0:2, j * 128:(j + 1) * 128], in_=spad[j][:, 512:514], identity=ident_f[:])
                return ii
            op("pe", fn, r=[("spadB", j) for j in range(4)] + ["ident_f"], w=[bk.res])
            op("act", lambda e: e.copy(out=ncp_sb[0:2, :], in_=bk.t[0:2, :]), r=[bk.res], w=["ncs_sb"])
            op("pool", lambda e: e.dma_start(out=ncp[:, :], in_=ncp_sb[0:2, :]), r=["ncs_sb"], w=["out_ncp"], dma=True)
            out_res.append("out_ncp")

    def B_groupB(B, j):
        N = B.N
        smp = B.sample
        bM = nbank()
        if not smp:
            def fn(pe):
                for (i, P) in B.tiles:
                    for half in range(2):
                        h = 2 * j + half
                        ii = pe.matmul(bM.t[64 * half:64 * half + 64, i * 128:(i + 1) * 128],
                                       lhsT=vn_bf[i][:, h * 64:(h + 1) * 64], rhs=Wt[:, h, :], start=True, stop=True)
                return ii
            op("pe", fn, r=[("vn", "p", i) for i in range(4)] + ["Wt"], w=[bM.res])
        else:
            def fn(pe):
                for half in range(2):
                    h = 2 * j + half
                    ii = pe.matmul(bM.t[64 * half:64 * half + 64, 0:64], lhsT=vn_s[0:64, h * 64:(h + 1) * 64],
                                   rhs=Wblk[0:64, h, :], start=True, stop=True)
                return ii
            op("pe", fn, r=[("vn", "s", 0)] + wblk_res, w=[bM.res])
        bU = mm_proj(B, 16 + j)
        bZ = mm_proj(B, 24 + j)
        sg, sgr = ntmp_for(B)
        tU, tUr = ntmp_for(B)
        mb, mbr = ntmp_for(B)
        op("act", lambda e: e.activation(out=sg[:, 0:N], in_=bZ.t[:, 0:N], func=AF.Silu), r=[bZ.res], w=[sgr])
        op("dve", lambda e: e.tensor_tensor(out=tU[:, 0:N], in0=bU.t[:, 0:N], in1=sg[:, 0:N], op=ALU.mult),
           r=[bU.res, sgr], w=[tUr])
        if not smp:
            op("dve", lambda e: e.scalar_tensor_tensor(out=mb[:, 0:N].rearrange("p (i t) -> p i t", t=128),
                                                       in0=bM.t[:, 0:N].rearrange("p (i t) -> p i t", t=128),
                                                       scalar=gvT[:, j:j + 1],
                                                       in1=bias2[:, j, :].unsqueeze(1).to_broadcast([128, 4, 128]),
                                                       op0=ALU.mult, op1=ALU.add),
               r=[bM.res, "gvT", "bias2"], w=[mbr])
        else:
            op("dve", lambda e: e.tensor_tensor(out=mb[:, 0:N].rearrange("p (s i) -> p s i", i=4),
                                                in0=bM.t[:, 0:N].rearrange("p (s i) -> p s i", i=4),
                                                in1=bsT[:, j, 0:4].unsqueeze(1).to_broadcast([128, 16, 4]), op=ALU.add),
               r=[bM.res, ("bsT", 2 * j), ("bsT", 2 * j + 1)], w=[mbr])
        op("pool", lambda e: e.tensor_tensor(out=B.yT[:, 4 + j, 0:N], in0=tU[:, 0:N], in1=mb[:, 0:N], op=ALU.mult),
           r=[tUr, mbr], w=[("yT", B.ykey, 4 + j)])

    c_state = {}

    def C_tile(B, ti):
        i, P = B.tiles[ti]
        smp = B.sample
        src = x_s[0:64, :] if smp else x_p[B.t0 + i * 128: B.t0 + (i + 1) * 128, :]
        dst = y_s[0:64, :] if smp else y_p[B.t0 + i * 128: B.t0 + (i + 1) * 128, :]
        gbc = gate_bc_s if smp else gate_bc_p
        gres = [("gate_bc_s", 0), ("gate_bc_s", 1)] if smp else [("gate_bc_p", 0), ("gate_bc_p", 1)]
        rc = cnt["xc"] % 2
        cnt["xc"] += 1
        ro = cnt["o"] % NOB
        cnt["o"] += 1
        op("sp", lambda e: e.dma_start(out=xc[rc][0:P, :], in_=src), w=[("xc", rc)], dma=True)
        b0 = nbank()
        b1 = nbank()

        def fn(pe):
            for k in range(8):
                pe.matmul(b0.t[0:P, :], lhsT=B.yT[:, k, i * 128:i * 128 + P], rhs=wout_bf[:, k, 0:512],
                          start=(k == 0), stop=(k == 7))
                ii = pe.matmul(b1.t[0:P, :], lhsT=B.yT[:, k, i * 128:i * 128 + P], rhs=wout_bf[:, k, 512:1024],
                               start=(k == 0), stop=(k == 7))
            return ii
        op("pe", fn, r=[("yT", B.ykey, k) for k in range(8)] + [("wout", 0), ("wout", 1)], w=[b0.res, b1.res])
        o = ob[ro]
        ores = ("o", ro)

        def fn(e):
            e.tensor_tensor(out=o[0:P, 0:512], in0=b0.t[0:P, :], in1=gbc[0:P, 0:512], op=ALU.mult)
            return e.tensor_tensor(out=o[0:P, 512:1024], in0=b1.t[0:P, :], in1=gbc[0:P, 512:1024], op=ALU.mult)
        op("dve", fn, r=[b0.res, b1.res] + gres, w=[ores])
        op("pool", lambda e: e.tensor_tensor(out=o[0:P, :], in0=o[0:P, :], in1=xc[rc][0:P, :], op=ALU.add),
           r=[ores, ("xc", rc)], w=[ores])
        sidx = nstat(2)
        ss = stat[0:P, sidx:sidx + 1]
        rs = stat[0:P, sidx + 1:sidx + 2]
        op("act", lambda e: e.activation(out=junk[0:P, :], in_=o[0:P, :], func=AF.Square, accum_out=ss),
           r=[ores], w=[("stat", sidx), "junk"])
        c_state[(B.idx, ti)] = (P, o, ores, sidx, ss, rs, dst)

    def C_tile2(B, ti):
        P, o, ores, sidx, ss, rs, dst = c_state[(B.idx, ti)]
        op("dve", lambda e: e.tensor_scalar(out=rs, in0=ss, scalar1=1.0 / D, scalar2=EPS, op0=ALU.mult, op1=ALU.add),
           r=[("stat", sidx)], w=[("stat", sidx + 1)])
        op("pool", lambda e: e.tensor_tensor(out=rs, in0=rs, in1=mhalf[0:P, 0:1], op=ALU.pow),
           r=[("stat", sidx + 1), "mhalf"], w=[("stat", sidx + 1)])
        op("dve", lambda e: e.scalar_tensor_tensor(out=o[0:P, :], in0=o[0:P, :], scalar=rs, in1=gf_bc[0:P, :],
                                                   op0=ALU.mult, op1=ALU.mult),
           r=[ores, ("stat", sidx + 1), "gf_bc"], w=[ores])
        ores_out = ("out_y", B.idx, ti)
        op("pool", lambda e: e.dma_start(out=dst, in_=o[0:P, :]), r=[ores], w=[ores_out], dma=True)
        out_res.append(ores_out)

    RIDE = 1
    A_pre(PB[0], 0)
    A_pre(PB[0], 1)
    small_loads_a()
    ada_pe(0)
    wada_dma(3)
    ada_pe(1)
    for m in m_order[:8]:
        win_dma(m)
    ada_pe(2)
    ada_pe(3)
    modT_ops()
    small_loads_b()
    A_post(PB[0], 0)
    A_pre(PB[0], 2)
    A_post(PB[0], 1)
    A_pre(PB[0], 3)
    A_post(PB[0], 2)
    A_post(PB[0], 3)
    for m in m_order[8:12]:
        win_dma(m)
    wada_dma(4)
    wada_dma(5)
    A_pre(PB[1], 0)
    startup_rest()
    B_vpath(PB[0])
    late_dma = [("win2", m) for m in (2, 10, 6, 14, 16, 24, 18, 26)] + [("wout", 0), ("wout", 1)]

    def win_dma2(m):
        op("pool", lambda e: e.dma_start(out=win_bf[:, :, m * 128:(m + 2) * 128],
                                         in_=w_in_v[:, :, m * 128:(m + 2) * 128]),
           w=[("win", m), ("win", m + 1)], dma=True)

    def issue_late(n):
        for _ in range(n):
            if late_dma:
                kind, m = late_dma.pop(0)
                if kind == "win":
                    win_dma(m)
                elif kind == "win2":
                    win_dma2(m)
                else:
                    wout_dma(m)
    for b in range(4):
        B = PB[b]
        NBk = PB[b + 1] if b + 1 < 4 else None
        rider = SB if b == RIDE else None
        for j in range(4):
            issue_late(5)
            B_groupA1(B, j)
            if b == RIDE - 1 and j == 2:
                A_post(SB, 0)
            if NBk is not None and j + 1 < 4:
                A_pre(NBk, j + 1)
            if NBk is not None:
                A_post(NBk, j)
            B_groupA2(B, j)
            if b == 0 and j == 0:
                gate_stage1()
            if b == 0 and j == 1:
                gate_stage2()
                bias2_ops()
            if b == 0 and j == 2:
                gate_stage3()
            if b == RIDE - 1 and j == 1:
                A_pre(SB, 0)
                cnt["xa"] += 1
            if b == RIDE - 1 and j == 3:
                B_vpath(SB)
            if rider is not None:
                B_groupA1(rider, j)
                B_groupA2(rider, j)
        B_newconv(B)
        if rider is not None:
            B_newconv(rider)
        for j in range(4):
            B_groupB(B, j)
            if rider is not None:
                B_groupB(rider, j)
            if j == 2 and b + 2 < 4:
                A_pre(PB[b + 2], 0)
        if NBk is not None:
            B_vpath(NBk)
        C_tile(B, 0)
        for ti in range(1, 4):
            C_tile(B, ti)
            C_tile2(B, ti - 1)
        if rider is not None:
            C_tile(rider, 0)
        C_tile2(B, 3)
        if rider is not None:
            C_tile2(rider, 0)

    op("sp", lambda e: e.nop(), r=out_res)
    stats = S.emit()
    return nc, stats


_CACHE = {}


def kernel(x_prompt, x_sample, state_conv, c_prompt, c_sample, w_ada, b_ada, g_norm, w_in, conv_w,
           g_v, b_v, w_s, b_s, w_out, g_final):
    if "nc" not in _CACHE:
        _CACHE["nc"] = build_nc()[0]
    nc = _CACHE["nc"]
    f = lambda a: np.ascontiguousarray(np.asarray(a, dtype=np.float32))
    x_prompt, x_sample, state_conv, c_prompt, c_sample = map(f, (x_prompt, x_sample, state_conv, c_prompt, c_sample))
    shared = {
        "w_ada": f(w_ada)[0], "b_ada": f(b_ada)[0], "g_norm": f(g_norm)[0], "w_in": f(w_in)[0],
        "conv_w": f(conv_w)[0], "g_v": f(g_v)[0], "b_v": f(b_v)[0], "w_s": f(w_s)[0], "b_s": f(b_s)[0],
        "w_out": f(w_out)[0], "g_final": f(g_final),
    }
    in_maps = []
    for i in range(NCORES):
        m = dict(shared)
        m["x_p"] = x_prompt[i]
        m["x_s"] = np.ascontiguousarray(x_sample[16 * i:16 * i + 16].reshape(NSAMP, D))
        m["state"] = np.ascontiguousarray(state_conv[0, 16 * i:16 * i + 16].reshape(32, 512))
        m["c_all"] = np.ascontiguousarray(np.concatenate([c_prompt[i:i + 1], c_sample[16 * i:16 * i + 16]], axis=0))
        in_maps.append(m)
    res = run_bass_kernel_spmd(nc, in_maps, core_ids=list(range(NCORES)))
    rs = res.results
    y_prompt = np.stack([rs[i]["y_p"] for i in range(NCORES)], axis=0)
    y_sample = np.concatenate([rs[i]["y_s"].reshape(16, 4, D) for i in range(NCORES)], axis=0)
    new_conv_prompt = np.stack([rs[i]["ncp"] for i in range(NCORES)], axis=0)[None]
    new_conv_sample = np.concatenate([rs[i]["ncs"].reshape(16, 2, 512) for i in range(NCORES)], axis=0)[None]
    new_v_sample = np.concatenate([rs[i]["nvs"].reshape(16, 4, 512) for i in range(NCORES)], axis=0)[None]
    return (y_prompt.astype(np.float32), y_sample.astype(np.float32), new_conv_prompt.astype(np.float32),
            new_conv_sample.astype(np.float32), new_v_sample.astype(np.float32))
```
